# Optimizing a Trainium2 kernel written in Bass

```python
import math
import jax, jax.numpy as jnp
from jax import lax
import numpy as np

D_MODEL = 1024
BATCH = 8
SEQ = 2048
DEPTH = 1

N_MEM = 256
RMS_EPS = 1e-6
RG_HEADS = 8
RG_HEAD_DIM = 64
RG_W = RG_HEADS * RG_HEAD_DIM
CONV_WIDTH = 4
LRU_C = 8.0
DA_HEADS = 4
DA_HEAD_DIM = 64
DA_V_DIM = 2 * DA_HEAD_DIM
DA_W = DA_HEADS * DA_V_DIM
LAYER_INDEX = 1
LAMBDA_INIT = 0.8 - 0.6 * math.exp(-0.3 * (LAYER_INDEX - 1))
Q_BLOCK = 128
XA_HEADS = 4
XA_HEAD_DIM = 128
XA_W = XA_HEADS * XA_HEAD_DIM
D_MIX = RG_W + DA_W + XA_W
IN_SPLITS = [RG_W, RG_W,
             DA_HEADS * 2 * DA_HEAD_DIM, DA_HEADS * 2 * DA_HEAD_DIM,
             DA_W, DA_W,
             XA_W, XA_W]
D_IN = sum(IN_SPLITS)

kernel_name = "hymba_style_rglru_diffattn_memxattn_layer"


def rms_norm(x, g, eps=RMS_EPS):
    xf = x.astype(jnp.float32)
    y = xf * lax.rsqrt(jnp.mean(xf * xf, axis=-1, keepdims=True) + eps)
    return (y * g.astype(jnp.float32)).astype(x.dtype)


def causal_depthwise_conv(x, w, b):
    s = x.shape[1]
    xp = jnp.pad(x, ((0, 0), (CONV_WIDTH - 1, 0), (0, 0)))
    y = b
    for k in range(CONV_WIDTH):
        y = y + w[k] * xp[:, k:k + s]
    return y


def rg_lru(xc, w_rg_a, b_rg_a, w_rg_x, b_rg_x, lru_lambda):
    b_, s, _ = xc.shape
    xh = xc.reshape(b_, s, RG_HEADS, RG_HEAD_DIM)
    r = jax.nn.sigmoid(jnp.einsum('bshi,hij->bshj', xh, w_rg_a)
                       + b_rg_a.reshape(RG_HEADS, RG_HEAD_DIM))
    i = jax.nn.sigmoid(jnp.einsum('bshi,hij->bshj', xh, w_rg_x)
                       + b_rg_x.reshape(RG_HEADS, RG_HEAD_DIM))
    r = r.reshape(b_, s, RG_W).astype(jnp.float32)
    i = i.reshape(b_, s, RG_W).astype(jnp.float32)
    log_a = -LRU_C * r * jax.nn.softplus(-lru_lambda.astype(jnp.float32))
    a = jnp.exp(log_a)
    mult = jnp.sqrt(jnp.maximum(1.0 - jnp.exp(2.0 * log_a), 0.0))
    u = mult * (i * xc.astype(jnp.float32))

    def combine(c1, c2):
        a1, b1 = c1
        a2, b2 = c2
        return a1 * a2, a2 * b1 + b2

    _, h = lax.associative_scan(combine, (a, u), axis=1)
    return h.astype(xc.dtype)


def diff_attention(q, k, v, lam):
    s = q.shape[1]
    scale = DA_HEAD_DIM ** -0.5
    q1, q2 = q[..., 0, :], q[..., 1, :]
    k1, k2 = k[..., 0, :], k[..., 1, :]
    outs = []
    for blk in range(s // Q_BLOCK):
        q0, kend = blk * Q_BLOCK, (blk + 1) * Q_BLOCK
        mask = (q0 + jnp.arange(Q_BLOCK))[:, None] >= jnp.arange(kend)[None, :]
        s1 = jnp.einsum('bqhd,bkhd->bhqk', q1[:, q0:kend], k1[:, :kend]).astype(jnp.float32) * scale
        s2 = jnp.einsum('bqhd,bkhd->bhqk', q2[:, q0:kend], k2[:, :kend]).astype(jnp.float32) * scale
        p1 = jax.nn.softmax(jnp.where(mask, s1, -jnp.inf), axis=-1)
        p2 = jax.nn.softmax(jnp.where(mask, s2, -jnp.inf), axis=-1)
        p = (p1 - lam * p2).astype(v.dtype)
        outs.append(jnp.einsum('bhqk,bkhe->bqhe', p, v[:, :kend]))
    return jnp.concatenate(outs, axis=1)


def mem_cross_attention(q, km, vm):
    sc = jnp.einsum('bshd,bmhd->bhsm', q, km).astype(jnp.float32) * (XA_HEAD_DIM ** -0.5)
    p = jax.nn.softmax(sc, axis=-1).astype(vm.dtype)
    return jnp.einsum('bhsm,bmhd->bshd', p, vm)


def setup_inputs(seed: int = 0) -> dict:
    key = jax.random.key(seed)
    ks = jax.random.split(key, 24)
    f32 = jnp.float32
    nrm = lambda k, shp, sc: jax.random.normal(k, shp, f32) * sc
    a0 = jax.random.uniform(ks[10], (RG_W,), f32, 0.9, 0.999)
    base = a0 ** (1.0 / LRU_C)
    lru_lambda = jnp.log(base / (1.0 - base))
    return {
        "x": nrm(ks[0], (BATCH, SEQ, D_MODEL), 1.0),
        "mem": nrm(ks[1], (BATCH, N_MEM, D_MODEL), 1.0),
        "g_pre": 1.0 + nrm(ks[2], (D_MODEL,), 0.02),
        "g_mem": 1.0 + nrm(ks[3], (D_MODEL,), 0.02),
        "w_in": nrm(ks[4], (D_MODEL, D_IN), D_MODEL ** -0.5),
        "w_mem_kv": nrm(ks[5], (D_MODEL, 2 * XA_W), D_MODEL ** -0.5),
        "conv_w": nrm(ks[6], (CONV_WIDTH, RG_W), CONV_WIDTH ** -0.5),
        "conv_b": nrm(ks[7], (RG_W,), 0.01),
        "w_rg_a": nrm(ks[8], (RG_HEADS, RG_HEAD_DIM, RG_HEAD_DIM), RG_HEAD_DIM ** -0.5),
        "b_rg_a": nrm(ks[9], (RG_W,), 0.01),
        "w_rg_x": nrm(ks[11], (RG_HEADS, RG_HEAD_DIM, RG_HEAD_DIM), RG_HEAD_DIM ** -0.5),
        "b_rg_x": nrm(ks[12], (RG_W,), 0.01),
        "lru_lambda": lru_lambda,
        "lambda_q1": nrm(ks[13], (DA_HEAD_DIM,), 0.1),
        "lambda_k1": nrm(ks[14], (DA_HEAD_DIM,), 0.1),
        "lambda_q2": nrm(ks[15], (DA_HEAD_DIM,), 0.1),
        "lambda_k2": nrm(ks[16], (DA_HEAD_DIM,), 0.1),
        "g_subln": 1.0 + nrm(ks[17], (DA_V_DIM,), 0.02),
        "w_out": nrm(ks[18], (D_MIX, D_MODEL), D_MIX ** -0.5),
        "g_post": 1.0 + nrm(ks[19], (D_MODEL,), 0.02),
    }


def reference(x, mem, g_pre, g_mem, w_in, w_mem_kv, conv_w, conv_b, w_rg_a, b_rg_a,
              w_rg_x, b_rg_x, lru_lambda, lambda_q1, lambda_k1, lambda_q2, lambda_k2,
              g_subln, w_out, g_post):
    b_, s, _ = x.shape
    memn = rms_norm(mem, g_mem)
    km, vm = jnp.split(memn @ w_mem_kv, 2, axis=-1)
    km = km.reshape(b_, N_MEM, XA_HEADS, XA_HEAD_DIM)
    vm = vm.reshape(b_, N_MEM, XA_HEADS, XA_HEAD_DIM)
    lam = (jnp.exp(jnp.sum(lambda_q1.astype(jnp.float32) * lambda_k1.astype(jnp.float32)))
           - jnp.exp(jnp.sum(lambda_q2.astype(jnp.float32) * lambda_k2.astype(jnp.float32)))
           + LAMBDA_INIT)

    for _ in range(DEPTH):
        hn = rms_norm(x, g_pre)
        proj = hn @ w_in
        cuts = np.cumsum(IN_SPLITS)[:-1].tolist()
        rg_x, rg_g, da_q, da_k, da_v, da_g, xa_q, xa_g = jnp.split(proj, cuts, axis=-1)

        xc = causal_depthwise_conv(rg_x, conv_w, conv_b)
        y_rg = rg_lru(xc, w_rg_a, b_rg_a, w_rg_x, b_rg_x, lru_lambda) * jax.nn.silu(rg_g)

        q = da_q.reshape(b_, s, DA_HEADS, 2, DA_HEAD_DIM)
        k = da_k.reshape(b_, s, DA_HEADS, 2, DA_HEAD_DIM)
        v = da_v.reshape(b_, s, DA_HEADS, DA_V_DIM)
        o = diff_attention(q, k, v, lam)
        o = rms_norm(o, g_subln) * (1.0 - LAMBDA_INIT)
        y_da = o.reshape(b_, s, DA_W) * jax.nn.silu(da_g)

        oc = mem_cross_attention(xa_q.reshape(b_, s, XA_HEADS, XA_HEAD_DIM), km, vm)
        y_xa = oc.reshape(b_, s, XA_W) * jax.nn.silu(xa_g)

        y = jnp.concatenate([y_rg, y_da, y_xa], axis=-1) @ w_out
        x = x + rms_norm(y, g_post)
    return x
```

```python
from contextlib import ExitStack
import numpy as np
import concourse.bass as bass
import concourse.mybir as mybir
from concourse.bass_utils import run_bass_kernel_spmd

F32 = mybir.dt.float32
BF16 = mybir.dt.bfloat16
ALU = mybir.AluOpType
AF = mybir.ActivationFunctionType
AX = mybir.AxisListType

D = 1024
NMEM = 256
EPS = 1e-6
LAMBDA_INIT = 0.2
NPK = 1344 + 2048


class Sched:
    COMPUTE = ("pe", "act", "dve", "pool")
    NDMA = 8

    def __init__(self, nc, stack):
        self.nc = nc
        self.eng = {"pe": nc.tensor, "act": nc.scalar, "dve": nc.vector,
                    "pool": nc.gpsimd, "sp": nc.sync}
        self.streams = {e: [] for e in self.eng}
        self.count = {e: 0 for e in self.COMPUTE}
        self.sem = {e: stack.enter_context(nc.semaphore("s_" + e)) for e in self.COMPUTE}
        self.dsem = {}
        self.dcount = {}
        for q in ("sp", "pool"):
            self.dsem[q] = [stack.enter_context(nc.semaphore("d_%s%d" % (q, i)))
                            for i in range(self.NDMA)]
            self.dcount[q] = 0
        self.waited = {e: {} for e in self.eng}
        self.res = {}

    def _deps(self, reads, writes):
        deps = []
        for r in reads:
            st = self.res.get(r)
            if st and st["w"] is not None:
                deps.append(st["w"])
        for w in writes:
            st = self.res.get(w)
            if st:
                if st["w"] is not None:
                    deps.append(st["w"])
                deps.extend(st["r"])
        return deps

    def _record(self, token, reads, writes):
        for r in reads:
            st = self.res.setdefault(r, {"w": None, "r": []})
            st["r"].append(token)
        for w in writes:
            self.res[w] = {"w": token, "r": []}

    def _waits(self, engine, deps):
        need = {}
        for (sem_id, sem, val) in deps:
            if engine == "pe" and sem_id == "pe":
                continue
            if self.waited[engine].get(sem_id, 0) >= val:
                continue
            if need.get(sem_id, (None, 0))[1] < val:
                need[sem_id] = (sem, val)
        out = []
        for sem_id, (sem, val) in need.items():
            self.waited[engine][sem_id] = val
            out.append((sem, val))
        return out

    def op(self, engine, fn, reads=(), writes=()):
        deps = self._deps(reads, writes)
        waits = self._waits(engine, deps)
        self.count[engine] += 1
        token = (engine, self.sem[engine], self.count[engine])
        self.streams[engine].append((waits, fn, (self.sem[engine], 1)))
        self._record(token, reads, writes)
        return token

    def dma(self, queue, fn, reads=(), writes=()):
        deps = self._deps(reads, writes)
        i = self.dcount[queue]
        self.dcount[queue] += 1
        slot = i % self.NDMA
        sem = self.dsem[queue][slot]
        sem_id = "d_%s%d" % (queue, slot)
        prev = 16 * (i // self.NDMA)
        if prev > 0:
            deps.append((sem_id, sem, prev))
        waits = self._waits(queue, deps)
        token = (sem_id, sem, prev + 16)
        self.streams[queue].append((waits, fn, (sem, 16)))
        self._record(token, reads, writes)
        return token

    def wait_all(self, engine, tokens):
        waits = self._waits(engine, list(tokens))
        self.streams[engine].append((waits, None, None))

    def emit(self, block):
        def run(name):
            def body(eng):
                for waits, fn, inc in self.streams[name]:
                    for sem, val in waits:
                        eng.wait_ge(sem, val)
                    if fn is not None:
                        ins = fn(eng)
                        ins.then_inc(inc[0], inc[1])
            return body
        block.tensor(run("pe"))
        block.scalar(run("act"))
        block.vector(run("dve"))
        block.gpsimd(run("pool"))
        block.sync(run("sp"))


def build_nc(S=2048):
    NT = S // 128
    NB = S // 512
    nc = bass.Bass("TRN2", target_bir_lowering=False)
    x_d = nc.dram_tensor("x", [S, D], F32, kind="ExternalInput").ap()
    mem_d = nc.dram_tensor("mem", [NMEM, D], F32, kind="ExternalInput").ap()
    win_d = nc.dram_tensor("w_in", [D, 4096], F32, kind="ExternalInput").ap()
    wkv_d = nc.dram_tensor("w_kv", [D, 1024], F32, kind="ExternalInput").ap()
    wout_d = nc.dram_tensor("w_out", [1536, D], F32, kind="ExternalInput").ap()
    pk_d = nc.dram_tensor("pk", [128, NPK], F32, kind="ExternalInput").ap()
    wga_d = nc.dram_tensor("wga", [128, 4, 128], F32, kind="ExternalInput").ap()
    wgx_d = nc.dram_tensor("wgx", [128, 4, 128], F32, kind="ExternalInput").ap()
    ident_d = nc.dram_tensor("ident", [128, 128], F32, kind="ExternalInput").ap()
    mask_d = nc.dram_tensor("mask", [128, 128], F32, kind="ExternalInput").ap()
    out_d = nc.dram_tensor("out", [S, D], F32, kind="ExternalOutput").ap()

    with ExitStack() as st:
        def sb(name, shape, dt):
            return st.enter_context(nc.sbuf_tensor("sb_" + name, shape, dt))

        def ps(name, shape, dt):
            return st.enter_context(nc.psum_tensor("ps_" + name, shape, dt))

        pk = sb("pk", [128, NPK], F32)
        dv = sb("dv", [128, 48], F32)
        idb = sb("idb", [128, 128], BF16)
        maskb = sb("maskb", [128, 128], BF16)
        wga = sb("wga_b", [128, 4, 128], BF16)
        wgx = sb("wgx_b", [128, 4, 128], BF16)
        NWB = 2
        wbuf = [sb("wbuf%d" % i, [128, 8, 512], BF16) for i in range(NWB)]
        NXS = 3
        xs = [sb("xs%d" % i, [128, D], F32) for i in range(NXS)]
        xb = [sb("xb%d" % i, [128, D], BF16) for i in range(NXS)]
        arena = sb("arena", [128, 8 * S], BF16)
        xT = arena[:].rearrange("p (c t) -> p c t", c=8)
        wo = arena[:, 0:12 * 1024].rearrange("p (c n) -> p c n", c=12)
        memnT = sb("memnT", [128, 8, NMEM], BF16)
        kmT = sb("kmT", [128, 4, NMEM], BF16)
        vm = sb("vm", [128, 2, 4, 129], BF16)
        y = sb("y", [128, 12, S], BF16)
        rgxb = sb("rgxb", [128, 4, 3 + S], BF16)
        NRG = 2
        dg = sb("dg", [128, 16, 128], BF16)
        XCB = [sb("XCB%d" % i, [128, 512], BF16) for i in range(NRG)]
        TR = [sb("TR%d" % i, [128, 512], F32) for i in range(NRG)]
        TI = [sb("TI%d" % i, [128, 512], F32) for i in range(NRG)]
        AA = [sb("AA%d" % i, [128, 512], F32) for i in range(NRG)]
        HH = [sb("HH%d" % i, [128, 512], F32) for i in range(NRG)]
        hst = sb("hst", [128, 4], F32)
        xq = sb("xq", [128, S], BF16)
        Q1 = sb("Q1", [128, S], BF16)
        Q2 = sb("Q2", [128, S], BF16)
        KK = sb("KK", [128, S], BF16)
        resb = Q1[:].bitcast(F32)
        VV = sb("VV", [128, NT, 129], BF16)
        NPT = 3
        PT = [sb("PT%d" % i, [128, 512], BF16) for i in range(NPT)]
        O1n = sb("O1n", [128, 4, 128], F32)
        oc = sb("oc", [128, 4, 128], F32)
        sq = sb("sq", [128, 4, 128], F32)
        onb = sb("onb", [128, 4, 128], BF16)
        rd = sb("rd", [128, 16], F32)
        sqb = sb("sqb", [128, 512], BF16)
        sst = sb("sst", [128, 16], F32)

        BK = [ps("B%d" % i, [128, 512], F32) for i in range(8)]
        MM = BK
        ST = [BK[2], BK[3], BK[6]]
        STK = [("MM", 2), ("MM", 3), ("MM", 6)]
        NST = 3

        def o_bank(s_, half):
            return 4 + 2 * s_ + half

        def o_slice(s_, qi):
            return BK[o_bank(s_, qi // 2)][:, (qi % 2) * 129:(qi % 2) * 129 + 129]

        def o_view(s_, half):
            return BK[o_bank(s_, half)][:, 0:258].rearrange("p (t n) -> p t n", n=129)

        def o_keys(s_):
            return [("MM", o_bank(s_, 0)), ("MM", o_bank(s_, 1))]
        mm_pool = [[0, 1]]

        S_ = Sched(nc, st)
        block = st.enter_context(nc.Block())
        mmi = [0]

        def next_mm():
            pool_ = mm_pool[0]
            b = pool_[mmi[0] % len(pool_)]
            mmi[0] += 1
            return b

        def pkc(a, b=None):
            return pk[:, a:(a + 1 if b is None else b)]
        G_PRE, G_MEM, CONVW, CONVB, BA, BX, LAM, GSUB = 0, 8, 16, 32, 36, 40, 44, 48
        LQ1, LK1, LQ2, LK2, GPOST = 64, 128, 192, 256, 320
        GPRE_BC, GMEM_BC = 1344, 2368
        HBA, HBX, CCH, HC, GS2, NLAM, T0, T1 = 0, 4, 8, 12, 16, 17, 18, 19

        S_.dma("sp", lambda q: q.dma_start(out=pk[:, GPRE_BC:GPRE_BC + D], in_=pk_d[:, GPRE_BC:GPRE_BC + D]),
               writes=["pk_pre"])
        S_.dma("pool", lambda q: q.dma_start(out=idb[:], in_=ident_d), writes=["idb"])

        def late_setup():
            S_.dma("pool", lambda q: q.dma_start(out=pk[:, 0:GPRE_BC], in_=pk_d[:, 0:GPRE_BC]), writes=["pk"])
            S_.dma("pool", lambda q: q.dma_start(out=pk[:, GMEM_BC:GMEM_BC + D], in_=pk_d[:, GMEM_BC:GMEM_BC + D]),
                   writes=["pk_mem"])
            S_.dma("pool", lambda q: q.dma_start(out=maskb[:], in_=mask_d), writes=["maskb"])
            S_.dma("pool", lambda q: q.dma_start(out=wga[:], in_=wga_d), writes=["wga"])
            S_.dma("pool", lambda q: q.dma_start(out=wgx[:], in_=wgx_d), writes=["wgx"])

        def late_setup_compute():
            S_.op("pool", lambda v: v.memset(VV[:, :, 128:129], 1.0), writes=[("VV", t4) for t4 in range(NB)])
            S_.op("pool", lambda v: v.memset(vm[:, :, :, 128:129], 1.0), writes=[("vm", 0), ("vm", 1)])
            S_.op("pool", lambda v: v.memset(rgxb[:, :, 0:3], 0.0), writes=[("rgxb", c, 0) for c in range(4)])
            S_.op("pool", lambda v: v.memset(Q1[:], 0.0), writes=[("Q1", j) for j in range(NB)])
            S_.op("pool", lambda v: v.memset(Q2[:], 0.0), writes=[("Q2", j) for j in range(NB)])
            S_.op("pool", lambda v: v.memset(hst[:], 0.0), writes=[("hst", c) for c in range(4)])
            for c_ in range(4):
                for k_ in range(4):
                    S_.op("pool", lambda v, c_=c_, k_=k_: v.tensor_scalar(
                        out=dg[:, c_ * 4 + k_, :], in0=idb[:], scalar1=pkc(CONVW + c_ * 4 + k_), scalar2=None,
                        op0=ALU.mult), reads=["idb", "pk"], writes=["dg"])
            S_.op("dve", lambda v: v.tensor_scalar(out=dv[:, HBA:HBA + 4], in0=pkc(BA, BA + 4), scalar1=-1.0,
                                                   scalar2=None, op0=ALU.mult), reads=["pk"], writes=["dv_hba"])
            S_.op("dve", lambda v: v.tensor_scalar(out=dv[:, HBX:HBX + 4], in0=pkc(BX, BX + 4), scalar1=-1.0,
                                                   scalar2=None, op0=ALU.mult), reads=["pk"], writes=["dv_hbx"])
            S_.op("dve", lambda v: v.tensor_scalar(out=dv[:, GS2:GS2 + 1], in0=pkc(GSUB), scalar1=(1.0 - LAMBDA_INIT),
                                                   scalar2=None, op0=ALU.mult), reads=["pk"], writes=["dv_gs"])
            S_.op("act", lambda a: a.activation(out=dv[:, T0 + 4:T0 + 8], in_=pkc(LAM, LAM + 4), func=AF.Exp, scale=-1.0),
                  reads=["pk"], writes=["dv_t"])
            S_.op("act", lambda a: a.activation(out=dv[:, T0 + 4:T0 + 8], in_=dv[:, T0 + 4:T0 + 8], func=AF.Ln,
                                                scale=1.0, bias=1.0), reads=["dv_t"], writes=["dv_t"])
            S_.op("dve", lambda v: v.tensor_scalar(out=dv[:, CCH:CCH + 4], in0=dv[:, T0 + 4:T0 + 8], scalar1=-8.0,
                                                   scalar2=None, op0=ALU.mult), reads=["dv_t"], writes=["dv_cch"])
            S_.op("dve", lambda v: v.tensor_scalar(out=dv[:, HC:HC + 4], in0=dv[:, T0 + 4:T0 + 8], scalar1=-16.0,
                                                   scalar2=None, op0=ALU.mult), reads=["dv_t"], writes=["dv_hc"])
            S_.op("dve", lambda v: v.tensor_tensor(out=sq[:, 0, 0:64], in0=pkc(LQ1, LQ1 + 64), in1=pkc(LK1, LK1 + 64),
                                                   op=ALU.mult), reads=["pk"], writes=["sq"])
            S_.op("dve", lambda v: v.reduce_sum(out=dv[:, T0:T0 + 1], in_=sq[:, 0, 0:64], axis=AX.X),
                  reads=["sq"], writes=["dv_s1"])
            S_.op("dve", lambda v: v.tensor_tensor(out=sq[:, 1, 0:64], in0=pkc(LQ2, LQ2 + 64), in1=pkc(LK2, LK2 + 64),
                                                   op=ALU.mult), reads=["pk"], writes=["sq"])
            S_.op("dve", lambda v: v.reduce_sum(out=dv[:, T1:T1 + 1], in_=sq[:, 1, 0:64], axis=AX.X),
                  reads=["sq"], writes=["dv_s2"])
            S_.op("act", lambda a: a.activation(out=dv[:, T0:T0 + 2], in_=dv[:, T0:T0 + 2], func=AF.Exp),
                  reads=["dv_s1", "dv_s2"], writes=["dv_e12"])
            S_.op("dve", lambda v: v.tensor_tensor(out=dv[:, NLAM:NLAM + 1], in0=dv[:, T1:T1 + 1], in1=dv[:, T0:T0 + 1],
                                                   op=ALU.subtract), reads=["dv_e12"], writes=["dv_nlam0"])
            S_.op("dve", lambda v: v.tensor_scalar(out=dv[:, NLAM:NLAM + 1], in0=dv[:, NLAM:NLAM + 1], scalar1=-LAMBDA_INIT,
                                                   scalar2=None, op0=ALU.add), reads=["dv_nlam0"], writes=["dv_nlam"])


        xs_i = [0]
        nt_pending = [None]

        def norm_transpose(src_ap, gcol, dst, dst_keys, ncol0):
            sl = xs_i[0] % NXS
            xs_i[0] += 1
            S_.dma("sp", lambda q: q.dma_start(out=xs[sl][:], in_=src_ap), writes=[("xs", sl)])
            S_.op("act", lambda a: a.activation(out=xb[sl][:], in_=xs[sl][:], func=AF.Square,
                                                accum_out=sst[:, sl:sl + 1]),
                  reads=[("xs", sl)], writes=[("xb", sl), ("ss", sl)])
            S_.op("act", lambda a: a.activation(out=sst[:, sl:sl + 1], in_=sst[:, sl:sl + 1], func=AF.Ln,
                                                scale=1.0 / D, bias=EPS), reads=[("ss", sl)], writes=[("ss", sl)])
            S_.op("act", lambda a: a.activation(out=sst[:, sl:sl + 1], in_=sst[:, sl:sl + 1], func=AF.Exp,
                                                scale=-0.5), reads=[("ss", sl)], writes=[("ss", sl)])
            S_.op("dve", lambda v: v.scalar_tensor_tensor(out=xb[sl][:], in0=xs[sl][:], scalar=sst[:, sl:sl + 1],
                                                          in1=pk[:, gcol:gcol + D], op0=ALU.mult, op1=ALU.mult),
                  reads=[("xs", sl), ("ss", sl), "pk_mem"], writes=[("xb", sl)])
            b = next_mm()
            pT = MM[b][:].bitcast(BF16).rearrange("p (c n) -> p c n", c=8)

            def tr(pe):
                ins = None
                for c in range(8):
                    ins = pe.transpose(pT[:, c, :], xb[sl][:, c * 128:(c + 1) * 128], idb[:])
                return ins
            S_.op("pe", tr, reads=[("xb", sl), "idb"], writes=[("MM", b)])
            use_act = (xs_i[0] % 2 == 0)

            def stage_b():
                if use_act:
                    S_.op("act", lambda a: a.activation(out=dst[:, :, ncol0:ncol0 + 128], in_=pT, func=AF.Copy),
                          reads=[("MM", b)], writes=dst_keys)
                else:
                    S_.op("dve", lambda v: v.tensor_copy(out=dst[:, :, ncol0:ncol0 + 128], in_=pT),
                          reads=[("MM", b)], writes=dst_keys)
            prev = nt_pending[0]
            nt_pending[0] = stage_b
            if prev is not None:
                prev()

        def norm_flush():
            if nt_pending[0] is not None:
                nt_pending[0]()
                nt_pending[0] = None

        wb_i = [0]
        win_v = win_d.rearrange("(c p) n -> p c n", p=128)

        def load_w(src):
            sl = wb_i[0] % NWB
            wb_i[0] += 1
            S_.dma("pool", lambda q: q.dma_start(out=wbuf[sl][:], in_=src), writes=[("wb", sl)])
            return sl

        def load_wgroup_cols(col0):
            return load_w(win_v[:, :, col0:col0 + 512])

        def load_wgroup_head(h):
            sl = wb_i[0] % NWB
            wb_i[0] += 1
            src = win_d[:, 1024:3072].rearrange("(c p) (s h n) -> p c s h n", p=128, s=4, h=4)
            dstv = wbuf[sl][:].rearrange("p c (s n) -> p c s n", s=4)
            for s in range(4):
                S_.dma("pool", lambda q, s=s: q.dma_start(out=dstv[:, :, s, :], in_=src[:, :, s, h, :]),
                       writes=[("wb", sl)] if s == 0 else [("wb", sl, s)], reads=[] if s == 0 else [])
            return sl

        def wkeys(sl, head=False):
            return [("wb", sl)] + ([("wb", sl, s) for s in (1, 2, 3)] if head else [])

        def proj_fm(sl, cc, j, head=False):
            b = next_mm()

            def mm(pe):
                ins = None
                for k in range(8):
                    ins = pe.matmul(MM[b][:], lhsT=wbuf[sl][:, k, cc * 128:(cc + 1) * 128],
                                    rhs=xT[:, k, j * 512:(j + 1) * 512], start=(k == 0), stop=(k == 7))
                return ins
            S_.op("pe", mm, reads=wkeys(sl, head) + [("xT", 4 * j + i) for i in range(4)], writes=[("MM", b)])
            return b

        sl0 = load_wgroup_cols(0)
        sl1 = load_wgroup_cols(512)
        late_setup()
        wkv_sb = y[:, 4:8, :].rearrange("p c (a n) -> p (c a) n", n=1024)
        wkv_keys = [("y", c, j) for c in range(4, 8) for j in range(NB)]
        S_.dma("pool", lambda q: q.dma_start(out=wkv_sb, in_=wkv_d.rearrange("(c p) n -> p c n", p=128)),
               writes=wkv_keys)
        mm_pool[0] = [0, 1, 2, 3, 4, 5, 6, 7]

        def xprep_stages(t, src_ap=None, gcol=None, gkey="pk_pre", dst=None, dkey=None, ncol0=None):
            sl = xs_i[0] % NXS
            xs_i[0] += 1
            ctx = {}
            if src_ap is None:
                src_ap, gcol, dst, dkey, ncol0 = x_d[t * 128:(t + 1) * 128, :], GPRE_BC, xT, ("xT", t), t * 128

            def A1():
                S_.dma("sp", lambda q: q.dma_start(out=xs[sl][:], in_=src_ap), writes=[("xs", sl)])
                S_.op("act", lambda a: a.activation(out=xb[sl][:], in_=xs[sl][:], func=AF.Square,
                                                    accum_out=sst[:, sl:sl + 1]),
                      reads=[("xs", sl)], writes=[("xb", sl), ("ss", sl)])
                S_.op("act", lambda a: a.activation(out=sst[:, sl:sl + 1], in_=sst[:, sl:sl + 1], func=AF.Ln,
                                                    scale=1.0 / D, bias=EPS), reads=[("ss", sl)], writes=[("ss", sl)])
                S_.op("act", lambda a: a.activation(out=sst[:, sl:sl + 1], in_=sst[:, sl:sl + 1], func=AF.Exp,
                                                    scale=-0.5), reads=[("ss", sl)], writes=[("ss", sl)])
                S_.op("dve", lambda v: v.scalar_tensor_tensor(out=xb[sl][:], in0=xs[sl][:], scalar=sst[:, sl:sl + 1],
                                                              in1=pk[:, gcol:gcol + D], op0=ALU.mult,
                                                              op1=ALU.mult),
                      reads=[("xs", sl), ("ss", sl), gkey], writes=[("xb", sl)])

            def A2():
                b = next_mm()
                ctx["b"] = b
                pT = MM[b][:].bitcast(BF16).rearrange("p (c n) -> p c n", c=8)
                ctx["pT"] = pT

                def tr(pe):
                    ins = None
                    for c in range(8):
                        ins = pe.transpose(pT[:, c, :], xb[sl][:, c * 128:(c + 1) * 128], idb[:])
                    return ins
                S_.op("pe", tr, reads=[("xb", sl), "idb"], writes=[("MM", b)])

            def Bst():
                b, pT = ctx["b"], ctx["pT"]
                if t % 2 == 0:
                    S_.op("act", lambda a: a.activation(out=dst[:, :, ncol0:ncol0 + 128], in_=pT, func=AF.Copy),
                          reads=[("MM", b)], writes=[dkey])
                else:
                    S_.op("dve", lambda v: v.tensor_copy(out=dst[:, :, ncol0:ncol0 + 128], in_=pT),
                          reads=[("MM", b)], writes=[dkey])
            return A1, A2, Bst

        gcount = [0]

        def g0_step(c, j):
            b = proj_fm(sl0, c, j)
            gcount[0] += 1
            if gcount[0] % 2 == 0:
                S_.op("dve", lambda v: v.tensor_copy(out=rgxb[:, c, 3 + j * 512:3 + (j + 1) * 512], in_=MM[b][:]),
                      reads=[("MM", b)], writes=[("rgxb", c, j + 1)])
            else:
                S_.op("act", lambda a: a.activation(out=rgxb[:, c, 3 + j * 512:3 + (j + 1) * 512], in_=MM[b][:],
                                                    func=AF.Copy), reads=[("MM", b)], writes=[("rgxb", c, j + 1)])

        def g1_step(c, j):
            b = proj_fm(sl1, c, j)
            gcount[0] += 1
            if gcount[0] % 2 == 0:
                S_.op("dve", lambda v: v.tensor_copy(out=y[:, c, j * 512:(j + 1) * 512], in_=MM[b][:]),
                      reads=[("MM", b)], writes=[("y", c, j)])
            else:
                S_.op("act", lambda a: a.activation(out=y[:, c, j * 512:(j + 1) * 512], in_=MM[b][:], func=AF.Copy),
                      reads=[("MM", b)], writes=[("y", c, j)])

        stages = [xprep_stages(t) for t in range(NT)]
        for mt in range(2):
            stages.append(xprep_stages(NT + mt, mem_d[mt * 128:(mt + 1) * 128, :], GMEM_BC, "pk_mem", memnT[:],
                                       ("memnT", mt), mt * 128))
        NTS = len(stages)
        mains = []
        for j in range(NB):
            mains.append((j, [(lambda c=c, j=j: g0_step(c, j)) for c in range(4)] +
                             [(lambda c=c, j=j: g1_step(c, j)) for c in range(4)]))
        ready = []
        for s_slot in range(NTS + 2):
            if 0 <= s_slot - 2 < NTS:
                stages[s_slot - 2][2]()
                if (s_slot - 2) % 4 == 3 and s_slot - 2 < NT:
                    ready.extend(mains[(s_slot - 2) // 4][1])
            if 0 <= s_slot - 1 < NTS:
                stages[s_slot - 1][1]()
            if s_slot < NTS:
                stages[s_slot][0]()
            for _ in range(2):
                if ready:
                    ready.pop(0)()
        while ready:
            ready.pop(0)()
        mm_pool[0] = [0, 1]
        mmi[0] = 0

        late_setup_compute()
        for h in range(4):
            b = next_mm()

            def mmk(pe, h=h, b=b):
                ins = None
                for c in range(8):
                    ins = pe.matmul(MM[b][:, 0:NMEM], lhsT=wkv_sb[:, c, h * 128:(h + 1) * 128],
                                    rhs=memnT[:, c, :], start=(c == 0), stop=(c == 7))
                return ins
            S_.op("pe", mmk, reads=wkv_keys + [("memnT", 0), ("memnT", 1)], writes=[("MM", b)])
            S_.op("dve", lambda v, h=h, b=b: v.tensor_copy(out=kmT[:, h, :], in_=MM[b][:, 0:NMEM]),
                  reads=[("MM", b)], writes=[("kmT", h)])
        for mt in range(2):
            b = next_mm()

            def mmv(pe, mt=mt, b=b):
                ins = None
                for c in range(8):
                    ins = pe.matmul(MM[b][:], lhsT=memnT[:, c, mt * 128:(mt + 1) * 128],
                                    rhs=wkv_sb[:, c, 512:1024], start=(c == 0), stop=(c == 7))
                return ins
            S_.op("pe", mmv, reads=wkv_keys + [("memnT", mt)], writes=[("MM", b)])
            S_.op("dve", lambda v, mt=mt, b=b: v.tensor_copy(
                out=vm[:, mt, :, 0:128], in_=MM[b][:].rearrange("p (h n) -> p h n", h=4)),
                reads=[("MM", b), ("vm", mt)], writes=[("vm", mt)])

        def rg_s1(i):
            c, j = rg_steps[i]
            u = i % NRG
            base = 3 + j * 512
            rk = [("rgxb", c, j), ("rgxb", c, j + 1)]
            bC = next_mm()

            def conv(pe):
                ins = None
                for k in range(4):
                    ins = pe.matmul(MM[bC][:], lhsT=dg[:, c * 4 + k, :], rhs=rgxb[:, c, base - 3 + k:base - 3 + k + 512],
                                    start=(k == 0), stop=(k == 3))
                return ins
            S_.op("pe", conv, reads=rk + ["dg"], writes=[("MM", bC)])
            S_.op("dve", lambda v: v.tensor_scalar(out=XCB[u][:], in0=MM[bC][:], scalar1=pkc(CONVB + c), scalar2=None,
                                                   op0=ALU.add), reads=[("MM", bC), "pk"], writes=[("XCB", u)])

        rg_half_pending = [None]

        rg_s3b_pending = []

        def rg_half():
            while rg_s3b_pending:
                rg_s3b_pending.pop(0)()
            if rg_half_pending[0] is not None:
                for f in rg_half_pending[0]:
                    f()
                rg_half_pending[0] = None

        def rg_s2(i):
            c, j = rg_steps[i]
            u = i % NRG
            bR = next_mm()
            S_.op("pe", lambda pe: pe.matmul(MM[bR][:], lhsT=wga[:, c, :], rhs=XCB[u][:], start=True, stop=True),
                  reads=["wga", ("XCB", u)], writes=[("MM", bR)])
            bI = next_mm()
            S_.op("pe", lambda pe: pe.matmul(MM[bI][:], lhsT=wgx[:, c, :], rhs=XCB[u][:], start=True, stop=True),
                  reads=["wgx", ("XCB", u)], writes=[("MM", bI)])
            acts = []

            def A(out, in_, key_r, key_w, extra=(), **kw):
                acts.append(lambda: S_.op("act", lambda a: a.activation(out=out, in_=in_, **kw),
                                          reads=[key_r] + list(extra), writes=[key_w]))
            tr, ti, aa = TR[u][:], TI[u][:], AA[u][:]
            kr, ki, ka = ("TR", u), ("TI", u), ("AA", u)
            A(tr, MM[bR][:], ("MM", bR), kr, ["dv_hba"], func=AF.Exp, scale=-1.0, bias=dv[:, HBA + c:HBA + c + 1])
            A(ti, MM[bI][:], ("MM", bI), ki, ["dv_hbx"], func=AF.Exp, scale=-1.0, bias=dv[:, HBX + c:HBX + c + 1])
            A(tr, tr, kr, kr, func=AF.Ln, scale=1.0, bias=1.0)
            A(tr, tr, kr, kr, func=AF.Exp, scale=-1.0)
            A(aa, tr, kr, ka, ["dv_cch"], func=AF.Exp, scale=dv[:, CCH + c:CCH + c + 1])
            A(tr, tr, kr, kr, ["dv_hc"], func=AF.Exp, scale=dv[:, HC + c:HC + c + 1])
            A(tr, tr, kr, kr, func=AF.Ln, scale=-1.0, bias=1.0)
            A(tr, tr, kr, kr, func=AF.Exp, scale=0.5)
            A(ti, ti, ki, ki, func=AF.Ln, scale=1.0, bias=1.0)
            A(ti, ti, ki, ki, func=AF.Exp, scale=-1.0)
            for f in acts[:5]:
                f()
            rg_half_pending[0] = acts[5:]

        def rg_s3(i):
            c, j = rg_steps[i]
            u = i % NRG
            S_.op("dve", lambda v: v.tensor_tensor(out=TI[u][:], in0=TI[u][:], in1=XCB[u][:], op=ALU.mult),
                  reads=[("TI", u), ("XCB", u)], writes=[("TI", u)])
            S_.op("dve", lambda v: v.tensor_tensor(out=TI[u][:], in0=TI[u][:], in1=TR[u][:], op=ALU.mult),
                  reads=[("TI", u), ("TR", u)], writes=[("TI", u)])
            def s3b():
                S_.op("dve", lambda v: v.tensor_tensor_scan(out=HH[u][:], data0=AA[u][:], data1=TI[u][:],
                                                            initial=hst[:, c:c + 1], op0=ALU.mult, op1=ALU.add),
                      reads=[("AA", u), ("TI", u), ("hst", c)], writes=[("HH", u)])
                S_.op("dve", lambda v: v.tensor_copy(out=hst[:, c:c + 1], in_=HH[u][:, 511:512]),
                      reads=[("HH", u)], writes=[("hst", c)])
                S_.op("dve", lambda v: v.tensor_tensor(out=y[:, c, j * 512:(j + 1) * 512], in0=HH[u][:],
                                                       in1=y[:, c, j * 512:(j + 1) * 512], op=ALU.mult),
                      reads=[("HH", u), ("y", c, j)], writes=[("y", c, j)])
            rg_s3b_pending.append(s3b)

        rg_steps = [(c, j) for j in range(NB) for c in range(4)]

        pt_i = [0]

        def transpose_gate(ychunk, j, scalar_ap):
            b = next_mm()
            pT = MM[b][:].bitcast(BF16)

            def tr(pe):
                ins = None
                for qi in range(4):
                    ins = pe.transpose(pT[:, qi * 128:(qi + 1) * 128], onb[:, qi, :], idb[:])
                return ins
            S_.op("pe", tr, reads=["onb", "idb"], writes=[("MM", b)])
            ysl = y[:, ychunk, j * 512:(j + 1) * 512]
            if scalar_ap is None:
                S_.op("dve", lambda v: v.tensor_tensor(out=ysl, in0=pT[:, 0:512], in1=ysl, op=ALU.mult),
                      reads=[("MM", b), ("y", ychunk, j)], writes=[("y", ychunk, j)])
            else:
                S_.op("dve", lambda v: v.scalar_tensor_tensor(out=ysl, in0=pT[:, 0:512], scalar=scalar_ap, in1=ysl,
                                                              op0=ALU.mult, op1=ALU.mult),
                      reads=[("MM", b), ("y", ychunk, j), "dv_gs"], writes=[("y", ychunk, j)])

        def recip_den(s_, col0):
            for half in range(2):
                S_.op("dve", lambda v, half=half: v.reciprocal(
                    out=rd[:, col0 + 2 * half:col0 + 2 * half + 2], in_=o_view(s_, half)[:, :, 128]),
                    reads=[("MM", o_bank(s_, half))], writes=[("rd", col0, half)])

        post_pending = [None]

        def run_unit(main_fn, post_a, post_b):
            main_fn()
            prev = post_pending[0]
            if prev is not None:
                prev()
            post_a()
            post_pending[0] = post_b

        def post_flush():
            b1_flush()
            if post_pending[0] is not None:
                post_pending[0]()
                post_pending[0] = None

        def xq_evac(b, j):
            S_.op("dve", lambda v: v.tensor_copy(out=xq[:, j * 512:(j + 1) * 512], in_=MM[b][:]),
                  reads=[("MM", b)], writes=[("xq", j)])

        def xg_raw_evac(h1):
            def ev(b, j):
                S_.op("dve", lambda v: v.tensor_copy(out=y[:, 8 + h1, j * 512:(j + 1) * 512], in_=MM[b][:]),
                      reads=[("MM", b)], writes=[("y", 8 + h1, j)])
            return ev

        def xattn_head(h, slq, slg):
            if h == 0:
                mm_pool[0] = [2, 3, 4, 5, 6, 7]
                for j in range(NB):
                    b = proj_fm(slq, h, j)
                    xq_evac(b, j)
                rg_point()
                for j in range(NB):
                    b = proj_fm(slg, h, j)
                    S_.op("act", lambda a, b=b, j=j: a.activation(out=y[:, 8 + h, j * 512:(j + 1) * 512],
                                                                  in_=MM[b][:], func=AF.Silu),
                          reads=[("MM", b)], writes=[("y", 8 + h, j)])
                for c in range(4):
                    S_.op("act", lambda a, c=c: a.activation(out=y[:, c, :], in_=y[:, c, :], func=AF.Silu),
                          reads=[("y", c, j) for j in range(NB)], writes=[("y", c, j) for j in range(NB)])
            else:
                fill_emit(len(fill_q))
                rg_point()
                S_.op("act", lambda a: a.activation(out=y[:, 8 + h, :], in_=y[:, 8 + h, :], func=AF.Silu),
                      reads=[("y", 8 + h, j) for j in range(NB)], writes=[("y", 8 + h, j) for j in range(NB)])
            mm_pool[0] = [6, 7]
            if h + 1 < 4:
                for j in range(NB):
                    fill_q.extend(fm_pieces(slg, h + 1, j, xg_raw_evac(h + 1), 4))

            def mid():
                rg_point()
                fill_emit(8)
            for j in range(NB):
                run_unit(lambda j=j: xattn_main(h, j, mid), lambda j=j: xattn_post_a(h, j),
                         lambda j=j: transpose_gate(8 + h, j, None))
                if h + 1 < 4:
                    fill_q.extend(fm_pieces(slq, h + 1, j, xq_evac, 4))

        def xattn_main(h, j, mid_hook=None):
            if True:
                s_ = 0
                pts = []
                for mt in range(2):
                    S_.op("pe", lambda pe, mt=mt, j=j: pe.matmul(
                        ST[mt][:], lhsT=kmT[:, h, mt * 128:(mt + 1) * 128], rhs=xq[:, j * 512:(j + 1) * 512],
                        start=True, stop=True), reads=[("kmT", h), ("xq", j)], writes=[STK[mt]])
                    p = pt_i[0] % NPT
                    pt_i[0] += 1
                    pts.append(p)
                    S_.op("act", lambda a, mt=mt, p=p: a.activation(out=PT[p][:], in_=ST[mt][:], func=AF.Exp,
                                                                    scale=128.0 ** -0.5),
                          reads=[STK[mt]], writes=[("PT", p)])
                if mid_hook is not None:
                    mid_hook()
                for mt in range(2):
                    p = pts[mt]

                    def pv(pe, mt=mt, p=p, s_=s_):
                        ins = None
                        for qi in range(4):
                            ins = pe.matmul(o_slice(s_, qi), lhsT=PT[p][:, qi * 128:(qi + 1) * 128],
                                            rhs=vm[:, mt, h, :], start=(mt == 0 and qi % 2 == 0), stop=(mt == 1),
                                            skip_group_check=True)
                        return ins
                    S_.op("pe", pv, reads=[("vm", mt), ("PT", p)], writes=o_keys(s_))

        def xattn_post_a(h, j):
            if True:
                s_ = 0
                recip_den(s_, 8)
                for half in range(2):
                    S_.op("dve", lambda v, half=half, s_=s_: v.tensor_tensor(
                        out=onb[:, 2 * half:2 * half + 2, :], in0=o_view(s_, half)[:, :, 0:128],
                        in1=rd[:, 8 + 2 * half:8 + 2 * half + 2].unsqueeze(2).to_broadcast([128, 2, 128]), op=ALU.mult),
                        reads=[("MM", o_bank(s_, half)), ("rd", 8, half)], writes=["onb"])

        fill_q = []
        fill_bank = [0]

        def fill_emit(n):
            for _ in range(n):
                if fill_q:
                    fill_q.pop(0)()

        def q_evac(b, j):
            S_.op("dve", lambda v: v.tensor_copy(out=Q1[0:64, j * 512:(j + 1) * 512], in_=MM[b][0:64, :]),
                  reads=[("MM", b)], writes=[("Q1", j)])
            S_.op("dve", lambda v: v.tensor_copy(out=Q2[64:128, j * 512:(j + 1) * 512], in_=MM[b][64:128, :]),
                  reads=[("MM", b)], writes=[("Q2", j)])

        def k_evac(b, j):
            S_.op("dve", lambda v: v.tensor_copy(out=KK[:, j * 512:(j + 1) * 512], in_=MM[b][:]),
                  reads=[("MM", b)], writes=[("KK", j)])

        def v_evac(b, t4):
            S_.op("dve", lambda v: v.tensor_copy(
                out=VV[:, t4 * 4:(t4 + 1) * 4, 0:128], in_=MM[b][:].rearrange("p (t n) -> p t n", t=4)),
                reads=[("MM", b), ("VV", t4)], writes=[("VV", t4)])

        def fm_pieces(sl, cc, j, evac, npieces):
            ctx = {}
            per = 8 // npieces

            def piece(i):
                def f():
                    if i == 0:
                        ctx["b"] = fill_bank[0] % 2
                        fill_bank[0] += 1
                    b = ctx["b"]

                    def mm(pe):
                        ins = None
                        for k in range(i * per, (i + 1) * per):
                            ins = pe.matmul(MM[b][:], lhsT=wbuf[sl][:, k, cc * 128:(cc + 1) * 128],
                                            rhs=xT[:, k, j * 512:(j + 1) * 512], start=(k == 0), stop=(k == 7))
                        return ins
                    S_.op("pe", mm, reads=wkeys(sl, True) + [("xT", 4 * j + i_) for i_ in range(4)],
                          writes=[("MM", b)])
                    if i == npieces - 1:
                        evac(b, j)
                return f
            return [piece(i) for i in range(npieces)]

        def v_pieces(sl, t4, npieces):
            ctx = {}
            per = 32 // npieces

            def piece(i):
                def f():
                    if i == 0:
                        ctx["b"] = fill_bank[0] % 2
                        fill_bank[0] += 1
                    b = ctx["b"]

                    def mm(pe):
                        ins = None
                        for m in range(i * per, (i + 1) * per):
                            tt, k = m // 8, m % 8
                            t = t4 * 4 + tt
                            ins = pe.matmul(MM[b][:, tt * 128:(tt + 1) * 128], lhsT=xT[:, k, t * 128:(t + 1) * 128],
                                            rhs=wbuf[sl][:, k, 256:384], start=(k == 0), stop=(k == 7))
                        return ins
                    S_.op("pe", mm, reads=wkeys(sl, True) + [("xT", 4 * t4 + i_) for i_ in range(4)],
                          writes=[("MM", b)])
                    if i == npieces - 1:
                        v_evac(b, t4)
                return f
            return [piece(i) for i in range(npieces)]

        def block_pieces(sl, jb, fine):
            n_fm, n_v = (4, 8) if fine else (1, 1)
            return (fm_pieces(sl, 0, jb, q_evac, n_fm) + fm_pieces(sl, 1, jb, k_evac, n_fm) +
                    v_pieces(sl, jb, n_v))

        def g_raw_evac(h1):
            def ev(b, j):
                S_.op("dve", lambda v: v.tensor_copy(out=y[:, 4 + h1, j * 512:(j + 1) * 512], in_=MM[b][:]),
                      reads=[("MM", b)], writes=[("y", 4 + h1, j)])
            return ev

        def gate_pieces(sl, h1):
            out = []
            for j in range(NB):
                out += fm_pieces(sl, 3, j, g_raw_evac(h1), 4)
            return out

        def dattn_proj_upfront(h, sl):
            if h == 0:
                for j in range(NB):
                    b = proj_fm(sl, 3, j, head=True)
                    S_.op("act", lambda a, b=b, j=j: a.activation(out=y[:, 4 + h, j * 512:(j + 1) * 512],
                                                                  in_=MM[b][:], func=AF.Silu),
                          reads=[("MM", b)], writes=[("y", 4 + h, j)])
            else:
                S_.op("act", lambda a: a.activation(out=y[:, 4 + h, :], in_=y[:, 4 + h, :], func=AF.Silu),
                      reads=[("y", 4 + h, j) for j in range(NB)], writes=[("y", 4 + h, j) for j in range(NB)])
            for p in block_pieces(sl, 0, False):
                p()

        def dattn_unit(h, j, w, fill_rate=0):
            Qw = Q1 if w == 0 else Q2
            qk = ("Q1", j) if w == 0 else ("Q2", j)
            nk = 4 * j + 4
            s_ = 0
            state = {}

            def qk_mm(kt):
                c0 = max(0, kt - 4 * j) * 128
                sb_ = kt % NST

                def f(pe):
                    diag = kt >= 4 * j
                    ins = pe.matmul(ST[sb_][:, c0:512], lhsT=KK[:, kt * 128:(kt + 1) * 128],
                                    rhs=Qw[:, j * 512 + c0:(j + 1) * 512], start=True, stop=not diag)
                    if diag:
                        ins = pe.matmul(ST[sb_][:, c0:c0 + 128], lhsT=idb[:], rhs=maskb[:], start=False, stop=True)
                    return ins
                S_.op("pe", f, reads=[("KK", kt // 4), qk, "idb", "maskb"], writes=[STK[sb_]])

            def exp_step(kt):
                c0 = max(0, kt - 4 * j) * 128
                sb_ = kt % NST
                p = pt_i[0] % NPT
                pt_i[0] += 1
                state[kt] = p
                S_.op("act", lambda a: a.activation(out=PT[p][:, c0:512], in_=ST[sb_][:, c0:512], func=AF.Exp,
                                                    scale=0.125),
                      reads=[STK[sb_]], writes=[("PT", p)])

            def pv_mm(kt):
                i0 = max(0, kt - 4 * j)
                p = state[kt]

                def f(pe):
                    ins = None
                    for qi in range(i0, 4):
                        ins = pe.matmul(o_slice(s_, qi), lhsT=PT[p][:, qi * 128:(qi + 1) * 128], rhs=VV[:, kt, :],
                                        start=(kt == 0 and qi % 2 == 0), stop=(kt == 4 * j + qi),
                                        skip_group_check=True)
                    return ins
                S_.op("pe", f, reads=[("VV", kt // 4), ("PT", p)], writes=o_keys(s_))

            qk_mm(0)
            qk_mm(1)
            if fill_rate > 0:
                fill_emit(5)
            for kt in range(nk):
                if kt + 2 < nk:
                    qk_mm(kt + 2)
                exp_step(kt)
                if j == 0 or kt % 2 == 1:
                    fill_emit(fill_rate)
                pv_mm(kt)
                if kt == 1:
                    b1_flush()

        b1_pending = [None]

        def b1_flush():
            if b1_pending[0] is not None:
                f = b1_pending[0]
                b1_pending[0] = None
                f()

        def dattn_post_a(h, j, w):
            recip_den(0, 4 * w)
            if w == 0:
                for half in range(2):
                    S_.op("dve", lambda v, half=half: v.tensor_tensor(
                        out=O1n[:, 2 * half:2 * half + 2, :], in0=o_view(0, half)[:, :, 0:128],
                        in1=rd[:, 2 * half:2 * half + 2].unsqueeze(2).to_broadcast([128, 2, 128]), op=ALU.mult),
                        reads=[("MM", o_bank(0, half)), ("rd", 0, half)], writes=[("O1n", half)])
            else:
                S_.op("dve", lambda v: v.tensor_scalar(out=rd[:, 4:8], in0=rd[:, 4:8], scalar1=dv[:, NLAM:NLAM + 1],
                                                       scalar2=None, op0=ALU.mult),
                      reads=[("rd", 4, 0), ("rd", 4, 1), "dv_nlam"], writes=[("rd", 4, 0), ("rd", 4, 1)])
                for qi in range(4):
                    S_.op("dve", lambda v, qi=qi: v.scalar_tensor_tensor(
                        out=oc[:, qi, :], in0=o_view(0, qi // 2)[:, qi % 2, 0:128], scalar=rd[:, 4 + qi:5 + qi],
                        in1=O1n[:, qi, :], op0=ALU.mult, op1=ALU.add),
                        reads=[("MM", o_bank(0, qi // 2)), ("rd", 4, qi // 2), ("O1n", qi // 2)], writes=[("oc", qi)])
                def b1():
                    ock = [("oc", qi) for qi in range(4)]
                    S_.op("dve", lambda v: v.tensor_tensor(out=sq[:], in0=oc[:], in1=oc[:], op=ALU.mult),
                          reads=ock, writes=["sq"])
                    S_.op("dve", lambda v: v.reduce_sum(out=rd[:, 12:16], in_=sq[:], axis=AX.X), reads=["sq"],
                          writes=["rstd4"])
                    S_.op("act", lambda a: a.activation(out=rd[:, 12:16], in_=rd[:, 12:16], func=AF.Ln,
                                                        scale=1.0 / 128.0, bias=EPS), reads=["rstd4"], writes=["rstd4"])
                    S_.op("act", lambda a: a.activation(out=rd[:, 12:16], in_=rd[:, 12:16], func=AF.Exp, scale=-0.5),
                          reads=["rstd4"], writes=["rstd4"])
                    S_.op("dve", lambda v: v.tensor_tensor(out=onb[:], in0=oc[:],
                                                           in1=rd[:, 12:16].unsqueeze(2).to_broadcast([128, 4, 128]),
                                                           op=ALU.mult), reads=ock + ["rstd4"], writes=["onb"])
                b1_pending[0] = b1

        def dattn_post_b(h, j, w):
            if w == 0:
                return
            transpose_gate(4 + h, j, dv[:, GS2:GS2 + 1])

        NRU = len(rg_steps)
        rg_next = [0]

        def rg_some(n):
            for _ in range(n):
                rg_half()
                i = rg_next[0]
                if i >= NRU + 2:
                    return
                if 0 <= i - 2 < NRU:
                    rg_s3(i - 2)
                if 0 <= i - 1 < NRU:
                    rg_s2(i - 1)
                if i < NRU:
                    rg_s1(i)
                rg_next[0] += 1

        rg_pt = [0]

        def rg_point():
            if rg_pt[0] % 2 == 0:
                rg_some(1)
            else:
                rg_half()
            rg_pt[0] += 1

        slq = load_wgroup_cols(3072)
        slg = load_wgroup_cols(3584)
        for h in range(4):
            xattn_head(h, slq, slg)
        fill_emit(len(fill_q))
        sl_next = load_wgroup_head(0)
        for h in range(4):
            sl = sl_next
            mm_pool[0] = [2, 3, 4, 5, 6, 7]
            rg_some(1)
            dattn_proj_upfront(h, sl)
            rg_half()
            mm_pool[0] = [7]
            if h + 1 < 4:
                sl_next = load_wgroup_head(h + 1)
            for j in range(NB):
                rate = 0
                steps = 2 * (4 * j + 4)
                if j + 1 < NB:
                    fill_q.extend(block_pieces(sl, j + 1, True))
                    rate = max(1, -(-(len(fill_q) - 10) // steps))
                elif h + 1 < 4:
                    fill_q.extend(gate_pieces(sl_next, h + 1))
                    rate = 1
                for w in range(2):
                    run_unit(lambda h=h, j=j, w=w, rate=rate: dattn_unit(h, j, w, rate),
                             lambda h=h, j=j, w=w: dattn_post_a(h, j, w),
                             lambda h=h, j=j, w=w: dattn_post_b(h, j, w))
                fill_emit(len(fill_q))
        post_flush()

        mm_pool[0] = [0, 1, 2, 3, 4, 5, 6, 7]
        mmi[0] = 0
        all_xT = [("xT", t) for t in range(NT)]
        wout_v = wout_d.rearrange("(c p) n -> p c n", p=128)
        for cg in range(3):
            S_.dma("pool", lambda q, cg=cg: q.dma_start(out=wo[:, cg * 4:(cg + 1) * 4, :],
                                                         in_=wout_v[:, cg * 4:(cg + 1) * 4, :]),
                   writes=[("wo", cg)] + (all_xT if cg == 0 else []), reads=all_xT if cg > 0 else [])
        out_tokens = []

        def out_stage1(t):
            sl = xs_i[0] % NXS
            xs_i[0] += 1
            par = t % 2
            c0 = 8 + 3 * par
            S_.dma("sp", lambda q: q.dma_start(out=xs[sl][:], in_=x_d[t * 128:(t + 1) * 128, :]),
                   writes=[("xs", sl)])
            bb = [next_mm(), next_mm()]
            for half in range(2):
                b = bb[half]

                def mmo(pe, b=b, half=half):
                    ins = None
                    for c in range(12):
                        ins = pe.matmul(MM[b][:], lhsT=y[:, c, t * 128:(t + 1) * 128],
                                        rhs=wo[:, c, half * 512:(half + 1) * 512], start=(c == 0), stop=(c == 11))
                    return ins
                S_.op("pe", mmo, reads=[("wo", 0), ("wo", 1), ("wo", 2)] + [("y", c, t // 4) for c in range(12)],
                      writes=[("MM", b)])
                S_.op("act", lambda a, b=b, half=half: a.activation(
                    out=sqb[:], in_=MM[b][:], func=AF.Square, accum_out=sst[:, c0 + half:c0 + half + 1]),
                    reads=[("MM", b)], writes=["sqb", ("oss", par, half)])
            S_.op("dve", lambda v: v.tensor_tensor(out=sst[:, c0 + 2:c0 + 3], in0=sst[:, c0:c0 + 1],
                                                   in1=sst[:, c0 + 1:c0 + 2], op=ALU.add),
                  reads=[("oss", par, 0), ("oss", par, 1)], writes=[("oss2", par)])
            S_.op("act", lambda a: a.activation(out=sst[:, c0 + 2:c0 + 3], in_=sst[:, c0 + 2:c0 + 3], func=AF.Ln,
                                                scale=1.0 / D, bias=EPS), reads=[("oss2", par)], writes=[("oss2", par)])
            S_.op("act", lambda a: a.activation(out=sst[:, c0 + 2:c0 + 3], in_=sst[:, c0 + 2:c0 + 3], func=AF.Exp,
                                                scale=-0.5), reads=[("oss2", par)], writes=[("oss2", par)])
            return sl, bb, par, c0

        def out_stage2(t, sl, bb, par, c0):
            for half in range(2):
                b = bb[half]
                S_.op("dve", lambda v, b=b, half=half: v.scalar_tensor_tensor(
                    out=resb[:, half * 512:(half + 1) * 512], in0=MM[b][:], scalar=sst[:, c0 + 2:c0 + 3],
                    in1=pk[:, GPOST + half * 512:GPOST + (half + 1) * 512], op0=ALU.mult, op1=ALU.mult),
                    reads=[("MM", b), ("oss2", par), "pk"],
                    writes=[("resb", half)] + [("Q1", jj) for jj in range(NB)])
            S_.op("dve", lambda v: v.tensor_tensor(out=xs[sl][:], in0=resb[:], in1=xs[sl][:], op=ALU.add),
                  reads=[("resb", 0), ("resb", 1), ("xs", sl)], writes=[("xs", sl)])
            tok = S_.dma("sp", lambda q: q.dma_start(out=out_d[t * 128:(t + 1) * 128, :], in_=xs[sl][:]),
                         reads=[("xs", sl)])
            out_tokens.append(tok)

        prev_ctx = None
        for t in range(NT):
            if t < NT // 2:
                if t % 2 == 0:
                    rg_some(1)
                else:
                    rg_half()
            elif t == NT // 2:
                rg_some(NRU + 2)
            ctx_t = out_stage1(t)
            if prev_ctx is not None:
                out_stage2(t - 1, *prev_ctx)
            prev_ctx = ctx_t
            if t >= NT // 2:
                rg_half()
        out_stage2(NT - 1, *prev_ctx)
        S_.wait_all("sp", out_tokens)
        S_.emit(block)
    return nc


def _host_consts(inp):
    f = np.float32
    pk = np.zeros((128, NPK), f)
    pk[:, 0:8] = inp["g_pre"].reshape(8, 128).T
    pk[:, 8:16] = inp["g_mem"].reshape(8, 128).T
    cw = inp["conv_w"].reshape(4, 4, 128)
    pk[:, 16:32] = cw.transpose(2, 1, 0).reshape(128, 16)
    pk[:, 32:36] = inp["conv_b"].reshape(4, 128).T
    pk[:, 36:40] = inp["b_rg_a"].reshape(4, 128).T
    pk[:, 40:44] = inp["b_rg_x"].reshape(4, 128).T
    pk[:, 44:48] = inp["lru_lambda"].reshape(4, 128).T
    pk[:, 48] = inp["g_subln"]
    pk[:, 64:128] = inp["lambda_q1"][None, :]
    pk[:, 128:192] = inp["lambda_k1"][None, :]
    pk[:, 192:256] = inp["lambda_q2"][None, :]
    pk[:, 256:320] = inp["lambda_k2"][None, :]
    pk[:, 320:1344] = inp["g_post"][None, :]
    pk[:, 1344:2368] = inp["g_pre"][None, :]
    pk[:, 2368:3392] = inp["g_mem"][None, :]

    def blockdiag(w):
        o = np.zeros((128, 4, 128), f)
        for c in range(4):
            o[0:64, c, 0:64] = w[2 * c]
            o[64:128, c, 64:128] = w[2 * c + 1]
        return o
    ident = np.eye(128, dtype=f)
    mask = np.where(np.triu(np.ones((128, 128), f)) > 0, 0.0, -30000.0).astype(f)
    return pk, blockdiag(inp["w_rg_a"]), blockdiag(inp["w_rg_x"]), ident, mask


_NC_CACHE = {}


def kernel(**inputs):
    inp = {k: np.asarray(v) for k, v in inputs.items()}
    x = np.ascontiguousarray(inp["x"], dtype=np.float32)
    B, S, _ = x.shape
    pk, wga, wgx, ident, mask = _host_consts(inp)
    if S not in _NC_CACHE:
        _NC_CACHE[S] = build_nc(S)
    nc = _NC_CACHE[S]
    shared = dict(w_in=np.ascontiguousarray(inp["w_in"], dtype=np.float32),
                  w_kv=np.ascontiguousarray(inp["w_mem_kv"], dtype=np.float32),
                  w_out=np.ascontiguousarray(inp["w_out"], dtype=np.float32),
                  pk=pk, wga=wga, wgx=wgx, ident=ident, mask=mask)
    in_maps = []
    for b in range(B):
        m = dict(shared)
        m["x"] = x[b]
        m["mem"] = np.ascontiguousarray(inp["mem"][b], dtype=np.float32)
        in_maps.append(m)
    res = run_bass_kernel_spmd(nc, in_maps, core_ids=list(range(B)))
    return np.stack([np.asarray(r["out"]) for r in res.results], axis=0).astype(np.float32)
```

```python
from contextlib import ExitStack
import numpy as np
import concourse.bass as bass
import concourse.mybir as mybir
from concourse.bass_utils import run_bass_kernel_spmd

F32 = mybir.dt.float32
BF16 = mybir.dt.bfloat16
ALU = mybir.AluOpType
AF = mybir.ActivationFunctionType
AX = mybir.AxisListType

D = 1024
NMEM = 256
EPS = 1e-6
LAMBDA_INIT = 0.2
NPK = 1344 + 2048


class Sched:
    COMPUTE = ("pe", "act", "dve", "pool")
    NDMA = 8

    def __init__(self, nc, stack):
        self.nc = nc
        self.eng = {"pe": nc.tensor, "act": nc.scalar, "dve": nc.vector,
                    "pool": nc.gpsimd, "sp": nc.sync}
        self.streams = {e: [] for e in self.eng}
        self.count = {e: 0 for e in self.COMPUTE}
        self.sem = {e: stack.enter_context(nc.semaphore("s_" + e)) for e in self.COMPUTE}
        self.dsem = {}
        self.dcount = {}
        for q in ("sp", "pool"):
            self.dsem[q] = [stack.enter_context(nc.semaphore("d_%s%d" % (q, i)))
                            for i in range(self.NDMA)]
            self.dcount[q] = 0
        self.waited = {e: {} for e in self.eng}
        self.res = {}

    def _deps(self, reads, writes):
        deps = []
        for r in reads:
            st = self.res.get(r)
            if st and st["w"] is not None:
                deps.append(st["w"])
        for w in writes:
            st = self.res.get(w)
            if st:
                if st["w"] is not None:
                    deps.append(st["w"])
                deps.extend(st["r"])
        return deps

    def _record(self, token, reads, writes):
        for r in reads:
            st = self.res.setdefault(r, {"w": None, "r": []})
            st["r"].append(token)
        for w in writes:
            self.res[w] = {"w": token, "r": []}

    def _waits(self, engine, deps):
        need = {}
        for (sem_id, sem, val) in deps:
            if engine == "pe" and sem_id == "pe":
                continue
            if self.waited[engine].get(sem_id, 0) >= val:
                continue
            if need.get(sem_id, (None, 0))[1] < val:
                need[sem_id] = (sem, val)
        out = []
        for sem_id, (sem, val) in need.items():
            self.waited[engine][sem_id] = val
            out.append((sem, val))
        return out

    def op(self, engine, fn, reads=(), writes=()):
        deps = self._deps(reads, writes)
        waits = self._waits(engine, deps)
        self.count[engine] += 1
        token = (engine, self.sem[engine], self.count[engine])
        self.streams[engine].append((waits, fn, (self.sem[engine], 1)))
        self._record(token, reads, writes)
        return token

    def dma(self, queue, fn, reads=(), writes=()):
        deps = self._deps(reads, writes)
        i = self.dcount[queue]
        self.dcount[queue] += 1
        slot = i % self.NDMA
        sem = self.dsem[queue][slot]
        sem_id = "d_%s%d" % (queue, slot)
        prev = 16 * (i // self.NDMA)
        if prev > 0:
            deps.append((sem_id, sem, prev))
        waits = self._waits(queue, deps)
        token = (sem_id, sem, prev + 16)
        self.streams[queue].append((waits, fn, (sem, 16)))
        self._record(token, reads, writes)
        return token

    def wait_all(self, engine, tokens):
        waits = self._waits(engine, list(tokens))
        self.streams[engine].append((waits, None, None))

    def emit(self, block):
        def run(name):
            def body(eng):
                for waits, fn, inc in self.streams[name]:
                    for sem, val in waits:
                        eng.wait_ge(sem, val)
                    if fn is not None:
                        ins = fn(eng)
                        ins.then_inc(inc[0], inc[1])
            return body
        block.tensor(run("pe"))
        block.scalar(run("act"))
        block.vector(run("dve"))
        block.gpsimd(run("pool"))
        block.sync(run("sp"))


def build_nc(S=2048):
    NT = S // 128
    NB = S // 512
    nc = bass.Bass("TRN2", target_bir_lowering=False)
    x_d = nc.dram_tensor("x", [S, D], F32, kind="ExternalInput").ap()
    mem_d = nc.dram_tensor("mem", [NMEM, D], F32, kind="ExternalInput").ap()
    win_d = nc.dram_tensor("w_in", [D, 4096], F32, kind="ExternalInput").ap()
    wkv_d = nc.dram_tensor("w_kv", [D, 1024], F32, kind="ExternalInput").ap()
    wout_d = nc.dram_tensor("w_out", [1536, D], F32, kind="ExternalInput").ap()
    pk_d = nc.dram_tensor("pk", [128, NPK], F32, kind="ExternalInput").ap()
    wga_d = nc.dram_tensor("wga", [128, 4, 128], F32, kind="ExternalInput").ap()
    wgx_d = nc.dram_tensor("wgx", [128, 4, 128], F32, kind="ExternalInput").ap()
    ident_d = nc.dram_tensor("ident", [128, 128], F32, kind="ExternalInput").ap()
    mask_d = nc.dram_tensor("mask", [128, 128], F32, kind="ExternalInput").ap()
    out_d = nc.dram_tensor("out", [S, D], F32, kind="ExternalOutput").ap()

    with ExitStack() as st:
        def sb(name, shape, dt):
            return st.enter_context(nc.sbuf_tensor("sb_" + name, shape, dt))

        def ps(name, shape, dt):
            return st.enter_context(nc.psum_tensor("ps_" + name, shape, dt))

        pk = sb("pk", [128, NPK], F32)
        dv = sb("dv", [128, 48], F32)
        idb = sb("idb", [128, 128], BF16)
        maskb = sb("maskb", [128, 128], BF16)
        wga = sb("wga_b", [128, 4, 128], BF16)
        wgx = sb("wgx_b", [128, 4, 128], BF16)
        NWB = 2
        wbuf = [sb("wbuf%d" % i, [128, 8, 512], BF16) for i in range(NWB)]
        NXS = 3
        xs = [sb("xs%d" % i, [128, D], F32) for i in range(NXS)]
        xb = [sb("xb%d" % i, [128, D], BF16) for i in range(NXS)]
        arena = sb("arena", [128, 8 * S], BF16)
        xT = arena[:].rearrange("p (c t) -> p c t", c=8)
        wo = arena[:, 0:12 * 1024].rearrange("p (c n) -> p c n", c=12)
        memnT = sb("memnT", [128, 8, NMEM], BF16)
        kmT = sb("kmT", [128, 4, NMEM], BF16)
        vm = sb("vm", [128, 2, 4, 129], BF16)
        y = sb("y", [128, 12, S], BF16)
        rgxb = sb("rgxb", [128, 4, 3 + S], BF16)
        NRG = 2
        dg = sb("dg", [128, 16, 128], BF16)
        XCB = [sb("XCB%d" % i, [128, 512], BF16) for i in range(NRG)]
        TR = [sb("TR%d" % i, [128, 512], F32) for i in range(NRG)]
        TI = [sb("TI%d" % i, [128, 512], F32) for i in range(NRG)]
        AA = [sb("AA%d" % i, [128, 512], F32) for i in range(NRG)]
        HH = [sb("HH%d" % i, [128, 512], F32) for i in range(NRG)]
        hst = sb("hst", [128, 4], F32)
        xq = sb("xq", [128, S], BF16)
        Q1 = sb("Q1", [128, S], BF16)
        Q2 = sb("Q2", [128, S], BF16)
        KK = sb("KK", [128, S], BF16)
        resb = Q1[:].bitcast(F32)
        VV = sb("VV", [128, NT, 129], BF16)
        NPT = 3
        PT = [sb("PT%d" % i, [128, 512], BF16) for i in range(NPT)]
        O1n = sb("O1n", [128, 4, 128], F32)
        oc = sb("oc", [128, 4, 128], F32)
        sq = sb("sq", [128, 4, 128], F32)
        onb = sb("onb", [128, 4, 128], BF16)
        rd = sb("rd", [128, 16], F32)
        sqb = sb("sqb", [128, 512], BF16)
        sst = sb("sst", [128, 16], F32)

        BK = [ps("B%d" % i, [128, 512], F32) for i in range(8)]
        MM = BK
        ST = [BK[2], BK[3], BK[6]]
        STK = [("MM", 2), ("MM", 3), ("MM", 6)]
        NST = 3

        def o_bank(s_, half):
            return 4 + 2 * s_ + half

        def o_slice(s_, qi):
            return BK[o_bank(s_, qi // 2)][:, (qi % 2) * 129:(qi % 2) * 129 + 129]

        def o_view(s_, half):
            return BK[o_bank(s_, half)][:, 0:258].rearrange("p (t n) -> p t n", n=129)

        def o_keys(s_):
            return [("MM", o_bank(s_, 0)), ("MM", o_bank(s_, 1))]
        mm_pool = [[0, 1]]

        S_ = Sched(nc, st)
        block = st.enter_context(nc.Block())
        mmi = [0]

        def next_mm():
            pool_ = mm_pool[0]
            b = pool_[mmi[0] % len(pool_)]
            mmi[0] += 1
            return b

        def pkc(a, b=None):
            return pk[:, a:(a + 1 if b is None else b)]
        G_PRE, G_MEM, CONVW, CONVB, BA, BX, LAM, GSUB = 0, 8, 16, 32, 36, 40, 44, 48
        LQ1, LK1, LQ2, LK2, GPOST = 64, 128, 192, 256, 320
        GPRE_BC, GMEM_BC = 1344, 2368
        HBA, HBX, CCH, HC, GS2, NLAM, T0, T1 = 0, 4, 8, 12, 16, 17, 18, 19

        S_.dma("sp", lambda q: q.dma_start(out=pk[:, GPRE_BC:GPRE_BC + D], in_=pk_d[:, GPRE_BC:GPRE_BC + D]),
               writes=["pk_pre"])
        S_.dma("pool", lambda q: q.dma_start(out=idb[:], in_=ident_d), writes=["idb"])

        def late_setup():
            S_.dma("pool", lambda q: q.dma_start(out=pk[:, 0:GPRE_BC], in_=pk_d[:, 0:GPRE_BC]), writes=["pk"])
            S_.dma("pool", lambda q: q.dma_start(out=pk[:, GMEM_BC:GMEM_BC + D], in_=pk_d[:, GMEM_BC:GMEM_BC + D]),
                   writes=["pk_mem"])
            S_.dma("pool", lambda q: q.dma_start(out=maskb[:], in_=mask_d), writes=["maskb"])
            S_.dma("pool", lambda q: q.dma_start(out=wga[:], in_=wga_d), writes=["wga"])
            S_.dma("pool", lambda q: q.dma_start(out=wgx[:], in_=wgx_d), writes=["wgx"])

        def late_setup_compute():
            S_.op("pool", lambda v: v.memset(VV[:, :, 128:129], 1.0), writes=[("VV", t4) for t4 in range(NB)])
            S_.op("pool", lambda v: v.memset(vm[:, :, :, 128:129], 1.0), writes=[("vm", 0), ("vm", 1)])
            S_.op("pool", lambda v: v.memset(rgxb[:, :, 0:3], 0.0), writes=[("rgxb", c, 0) for c in range(4)])
            S_.op("pool", lambda v: v.memset(Q1[:], 0.0), writes=[("Q1", j) for j in range(NB)])
            S_.op("pool", lambda v: v.memset(Q2[:], 0.0), writes=[("Q2", j) for j in range(NB)])
            S_.op("pool", lambda v: v.memset(hst[:], 0.0), writes=[("hst", c) for c in range(4)])
            for c_ in range(4):
                for k_ in range(4):
                    S_.op("pool", lambda v, c_=c_, k_=k_: v.tensor_scalar(
                        out=dg[:, c_ * 4 + k_, :], in0=idb[:], scalar1=pkc(CONVW + c_ * 4 + k_), scalar2=None,
                        op0=ALU.mult), reads=["idb", "pk"], writes=["dg"])
            S_.op("dve", lambda v: v.tensor_scalar(out=dv[:, HBA:HBA + 4], in0=pkc(BA, BA + 4), scalar1=-1.0,
                                                   scalar2=None, op0=ALU.mult), reads=["pk"], writes=["dv_hba"])
            S_.op("dve", lambda v: v.tensor_scalar(out=dv[:, HBX:HBX + 4], in0=pkc(BX, BX + 4), scalar1=-1.0,
                                                   scalar2=None, op0=ALU.mult), reads=["pk"], writes=["dv_hbx"])
            S_.op("dve", lambda v: v.tensor_scalar(out=dv[:, GS2:GS2 + 1], in0=pkc(GSUB), scalar1=(1.0 - LAMBDA_INIT),
                                                   scalar2=None, op0=ALU.mult), reads=["pk"], writes=["dv_gs"])
            S_.op("act", lambda a: a.activation(out=dv[:, T0 + 4:T0 + 8], in_=pkc(LAM, LAM + 4), func=AF.Exp, scale=-1.0),
                  reads=["pk"], writes=["dv_t"])
            S_.op("act", lambda a: a.activation(out=dv[:, T0 + 4:T0 + 8], in_=dv[:, T0 + 4:T0 + 8], func=AF.Ln,
                                                scale=1.0, bias=1.0), reads=["dv_t"], writes=["dv_t"])
            S_.op("dve", lambda v: v.tensor_scalar(out=dv[:, CCH:CCH + 4], in0=dv[:, T0 + 4:T0 + 8], scalar1=-8.0,
                                                   scalar2=None, op0=ALU.mult), reads=["dv_t"], writes=["dv_cch"])
            S_.op("dve", lambda v: v.tensor_scalar(out=dv[:, HC:HC + 4], in0=dv[:, T0 + 4:T0 + 8], scalar1=-16.0,
                                                   scalar2=None, op0=ALU.mult), reads=["dv_t"], writes=["dv_hc"])
            S_.op("dve", lambda v: v.tensor_tensor(out=sq[:, 0, 0:64], in0=pkc(LQ1, LQ1 + 64), in1=pkc(LK1, LK1 + 64),
                                                   op=ALU.mult), reads=["pk"], writes=["sq"])
            S_.op("dve", lambda v: v.reduce_sum(out=dv[:, T0:T0 + 1], in_=sq[:, 0, 0:64], axis=AX.X),
                  reads=["sq"], writes=["dv_s1"])
            S_.op("dve", lambda v: v.tensor_tensor(out=sq[:, 1, 0:64], in0=pkc(LQ2, LQ2 + 64), in1=pkc(LK2, LK2 + 64),
                                                   op=ALU.mult), reads=["pk"], writes=["sq"])
            S_.op("dve", lambda v: v.reduce_sum(out=dv[:, T1:T1 + 1], in_=sq[:, 1, 0:64], axis=AX.X),
                  reads=["sq"], writes=["dv_s2"])
            S_.op("act", lambda a: a.activation(out=dv[:, T0:T0 + 2], in_=dv[:, T0:T0 + 2], func=AF.Exp),
                  reads=["dv_s1", "dv_s2"], writes=["dv_e12"])
            S_.op("dve", lambda v: v.tensor_tensor(out=dv[:, NLAM:NLAM + 1], in0=dv[:, T1:T1 + 1], in1=dv[:, T0:T0 + 1],
                                                   op=ALU.subtract), reads=["dv_e12"], writes=["dv_nlam0"])
            S_.op("dve", lambda v: v.tensor_scalar(out=dv[:, NLAM:NLAM + 1], in0=dv[:, NLAM:NLAM + 1], scalar1=-LAMBDA_INIT,
                                                   scalar2=None, op0=ALU.add), reads=["dv_nlam0"], writes=["dv_nlam"])


        xs_i = [0]
        nt_pending = [None]

        def norm_transpose(src_ap, gcol, dst, dst_keys, ncol0):
            sl = xs_i[0] % NXS
            xs_i[0] += 1
            S_.dma("sp", lambda q: q.dma_start(out=xs[sl][:], in_=src_ap), writes=[("xs", sl)])
            S_.op("act", lambda a: a.activation(out=xb[sl][:], in_=xs[sl][:], func=AF.Square,
                                                accum_out=sst[:, sl:sl + 1]),
                  reads=[("xs", sl)], writes=[("xb", sl), ("ss", sl)])
            S_.op("act", lambda a: a.activation(out=sst[:, sl:sl + 1], in_=sst[:, sl:sl + 1], func=AF.Ln,
                                                scale=1.0 / D, bias=EPS), reads=[("ss", sl)], writes=[("ss", sl)])
            S_.op("act", lambda a: a.activation(out=sst[:, sl:sl + 1], in_=sst[:, sl:sl + 1], func=AF.Exp,
                                                scale=-0.5), reads=[("ss", sl)], writes=[("ss", sl)])
            S_.op("dve", lambda v: v.scalar_tensor_tensor(out=xb[sl][:], in0=xs[sl][:], scalar=sst[:, sl:sl + 1],
                                                          in1=pk[:, gcol:gcol + D], op0=ALU.mult, op1=ALU.mult),
                  reads=[("xs", sl), ("ss", sl), "pk_mem"], writes=[("xb", sl)])
            b = next_mm()
            pT = MM[b][:].bitcast(BF16).rearrange("p (c n) -> p c n", c=8)

            def tr(pe):
                ins = None
                for c in range(8):
                    ins = pe.transpose(pT[:, c, :], xb[sl][:, c * 128:(c + 1) * 128], idb[:])
                return ins
            S_.op("pe", tr, reads=[("xb", sl), "idb"], writes=[("MM", b)])
            use_act = (xs_i[0] % 2 == 0)

            def stage_b():
                if use_act:
                    S_.op("act", lambda a: a.activation(out=dst[:, :, ncol0:ncol0 + 128], in_=pT, func=AF.Copy),
                          reads=[("MM", b)], writes=dst_keys)
                else:
                    S_.op("dve", lambda v: v.tensor_copy(out=dst[:, :, ncol0:ncol0 + 128], in_=pT),
                          reads=[("MM", b)], writes=dst_keys)
            prev = nt_pending[0]
            nt_pending[0] = stage_b
            if prev is not None:
                prev()

        def norm_flush():
            if nt_pending[0] is not None:
                nt_pending[0]()
                nt_pending[0] = None

        wb_i = [0]
        win_v = win_d.rearrange("(c p) n -> p c n", p=128)

        def load_w(src):
            sl = wb_i[0] % NWB
            wb_i[0] += 1
            S_.dma("pool", lambda q: q.dma_start(out=wbuf[sl][:], in_=src), writes=[("wb", sl)])
            return sl

        def load_wgroup_cols(col0):
            return load_w(win_v[:, :, col0:col0 + 512])

        def load_wgroup_head(h):
            sl = wb_i[0] % NWB
            wb_i[0] += 1
            src = win_d[:, 1024:3072].rearrange("(c p) (s h n) -> p c s h n", p=128, s=4, h=4)
            dstv = wbuf[sl][:].rearrange("p c (s n) -> p c s n", s=4)
            for s in range(4):
                S_.dma("pool", lambda q, s=s: q.dma_start(out=dstv[:, :, s, :], in_=src[:, :, s, h, :]),
                       writes=[("wb", sl)] if s == 0 else [("wb", sl, s)], reads=[] if s == 0 else [])
            return sl

        def wkeys(sl, head=False):
            return [("wb", sl)] + ([("wb", sl, s) for s in (1, 2, 3)] if head else [])

        def proj_fm(sl, cc, j, head=False):
            b = next_mm()

            def mm(pe):
                ins = None
                for k in range(8):
                    ins = pe.matmul(MM[b][:], lhsT=wbuf[sl][:, k, cc * 128:(cc + 1) * 128],
                                    rhs=xT[:, k, j * 512:(j + 1) * 512], start=(k == 0), stop=(k == 7))
                return ins
            S_.op("pe", mm, reads=wkeys(sl, head) + [("xT", 4 * j + i) for i in range(4)], writes=[("MM", b)])
            return b

        sl0 = load_wgroup_cols(0)
        sl1 = load_wgroup_cols(512)
        late_setup()
        wkv_sb = y[:, 4:8, :].rearrange("p c (a n) -> p (c a) n", n=1024)
        wkv_keys = [("y", c, j) for c in range(4, 8) for j in range(NB)]
        S_.dma("pool", lambda q: q.dma_start(out=wkv_sb, in_=wkv_d.rearrange("(c p) n -> p c n", p=128)),
               writes=wkv_keys)
        mm_pool[0] = [0, 1, 2, 3, 4, 5, 6, 7]

        def xprep_stages(t, src_ap=None, gcol=None, gkey="pk_pre", dst=None, dkey=None, ncol0=None):
            sl = xs_i[0] % NXS
            xs_i[0] += 1
            ctx = {}
            if src_ap is None:
                src_ap, gcol, dst, dkey, ncol0 = x_d[t * 128:(t + 1) * 128, :], GPRE_BC, xT, ("xT", t), t * 128

            def A1():
                S_.dma("sp", lambda q: q.dma_start(out=xs[sl][:], in_=src_ap), writes=[("xs", sl)])
                S_.op("act", lambda a: a.activation(out=xb[sl][:], in_=xs[sl][:], func=AF.Square,
                                                    accum_out=sst[:, sl:sl + 1]),
                      reads=[("xs", sl)], writes=[("xb", sl), ("ss", sl)])
                S_.op("act", lambda a: a.activation(out=sst[:, sl:sl + 1], in_=sst[:, sl:sl + 1], func=AF.Ln,
                                                    scale=1.0 / D, bias=EPS), reads=[("ss", sl)], writes=[("ss", sl)])
                S_.op("act", lambda a: a.activation(out=sst[:, sl:sl + 1], in_=sst[:, sl:sl + 1], func=AF.Exp,
                                                    scale=-0.5), reads=[("ss", sl)], writes=[("ss", sl)])
                S_.op("dve", lambda v: v.scalar_tensor_tensor(out=xb[sl][:], in0=xs[sl][:], scalar=sst[:, sl:sl + 1],
                                                              in1=pk[:, gcol:gcol + D], op0=ALU.mult,
                                                              op1=ALU.mult),
                      reads=[("xs", sl), ("ss", sl), gkey], writes=[("xb", sl)])

            def A2():
                b = next_mm()
                ctx["b"] = b
                pT = MM[b][:].bitcast(BF16).rearrange("p (c n) -> p c n", c=8)
                ctx["pT"] = pT

                def tr(pe):
                    ins = None
                    for c in range(8):
                        ins = pe.transpose(pT[:, c, :], xb[sl][:, c * 128:(c + 1) * 128], idb[:])
                    return ins
                S_.op("pe", tr, reads=[("xb", sl), "idb"], writes=[("MM", b)])

            def Bst():
                b, pT = ctx["b"], ctx["pT"]
                if t % 2 == 0:
                    S_.op("act", lambda a: a.activation(out=dst[:, :, ncol0:ncol0 + 128], in_=pT, func=AF.Copy),
                          reads=[("MM", b)], writes=[dkey])
                else:
                    S_.op("dve", lambda v: v.tensor_copy(out=dst[:, :, ncol0:ncol0 + 128], in_=pT),
                          reads=[("MM", b)], writes=[dkey])
            return A1, A2, Bst

        gcount = [0]

        def g0_step(c, j):
            b = proj_fm(sl0, c, j)
            gcount[0] += 1
            if gcount[0] % 2 == 0:
                S_.op("dve", lambda v: v.tensor_copy(out=rgxb[:, c, 3 + j * 512:3 + (j + 1) * 512], in_=MM[b][:]),
                      reads=[("MM", b)], writes=[("rgxb", c, j + 1)])
            else:
                S_.op("act", lambda a: a.activation(out=rgxb[:, c, 3 + j * 512:3 + (j + 1) * 512], in_=MM[b][:],
                                                    func=AF.Copy), reads=[("MM", b)], writes=[("rgxb", c, j + 1)])

        def g1_step(c, j):
            b = proj_fm(sl1, c, j)
            gcount[0] += 1
            if gcount[0] % 2 == 0:
                S_.op("dve", lambda v: v.tensor_copy(out=y[:, c, j * 512:(j + 1) * 512], in_=MM[b][:]),
                      reads=[("MM", b)], writes=[("y", c, j)])
            else:
                S_.op("act", lambda a: a.activation(out=y[:, c, j * 512:(j + 1) * 512], in_=MM[b][:], func=AF.Copy),
                      reads=[("MM", b)], writes=[("y", c, j)])

        stages = [xprep_stages(t) for t in range(NT)]
        for mt in range(2):
            stages.append(xprep_stages(NT + mt, mem_d[mt * 128:(mt + 1) * 128, :], GMEM_BC, "pk_mem", memnT[:],
                                       ("memnT", mt), mt * 128))
        NTS = len(stages)
        mains = []
        for j in range(NB):
            mains.append((j, [(lambda c=c, j=j: g0_step(c, j)) for c in range(4)] +
                             [(lambda c=c, j=j: g1_step(c, j)) for c in range(4)]))
        ready = []
        for s_slot in range(NTS + 2):
            if 0 <= s_slot - 2 < NTS:
                stages[s_slot - 2][2]()
                if (s_slot - 2) % 4 == 3 and s_slot - 2 < NT:
                    ready.extend(mains[(s_slot - 2) // 4][1])
            if 0 <= s_slot - 1 < NTS:
                stages[s_slot - 1][1]()
            if s_slot < NTS:
                stages[s_slot][0]()
            for _ in range(2):
                if ready:
                    ready.pop(0)()
        while ready:
            ready.pop(0)()
        mm_pool[0] = [0, 1]
        mmi[0] = 0

        late_setup_compute()
        for h in range(4):
            b = next_mm()

            def mmk(pe, h=h, b=b):
                ins = None
                for c in range(8):
                    ins = pe.matmul(MM[b][:, 0:NMEM], lhsT=wkv_sb[:, c, h * 128:(h + 1) * 128],
                                    rhs=memnT[:, c, :], start=(c == 0), stop=(c == 7))
                return ins
            S_.op("pe", mmk, reads=wkv_keys + [("memnT", 0), ("memnT", 1)], writes=[("MM", b)])
            S_.op("dve", lambda v, h=h, b=b: v.tensor_copy(out=kmT[:, h, :], in_=MM[b][:, 0:NMEM]),
                  reads=[("MM", b)], writes=[("kmT", h)])
        for mt in range(2):
            b = next_mm()

            def mmv(pe, mt=mt, b=b):
                ins = None
                for c in range(8):
                    ins = pe.matmul(MM[b][:], lhsT=memnT[:, c, mt * 128:(mt + 1) * 128],
                                    rhs=wkv_sb[:, c, 512:1024], start=(c == 0), stop=(c == 7))
                return ins
            S_.op("pe", mmv, reads=wkv_keys + [("memnT", mt)], writes=[("MM", b)])
            S_.op("dve", lambda v, mt=mt, b=b: v.tensor_copy(
                out=vm[:, mt, :, 0:128], in_=MM[b][:].rearrange("p (h n) -> p h n", h=4)),
                reads=[("MM", b), ("vm", mt)], writes=[("vm", mt)])

        def rg_s1(i):
            c, j = rg_steps[i]
            u = i % NRG
            base = 3 + j * 512
            rk = [("rgxb", c, j), ("rgxb", c, j + 1)]
            bC = next_mm()

            def conv(pe):
                ins = None
                for k in range(4):
                    ins = pe.matmul(MM[bC][:], lhsT=dg[:, c * 4 + k, :], rhs=rgxb[:, c, base - 3 + k:base - 3 + k + 512],
                                    start=(k == 0), stop=(k == 3))
                return ins
            S_.op("pe", conv, reads=rk + ["dg"], writes=[("MM", bC)])
            S_.op("dve", lambda v: v.tensor_scalar(out=XCB[u][:], in0=MM[bC][:], scalar1=pkc(CONVB + c), scalar2=None,
                                                   op0=ALU.add), reads=[("MM", bC), "pk"], writes=[("XCB", u)])

        rg_half_pending = [None]

        rg_s3b_pending = []

        def rg_half():
            while rg_s3b_pending:
                rg_s3b_pending.pop(0)()
            if rg_half_pending[0] is not None:
                for f in rg_half_pending[0]:
                    f()
                rg_half_pending[0] = None

        def rg_s2(i):
            c, j = rg_steps[i]
            u = i % NRG
            bR = next_mm()
            S_.op("pe", lambda pe: pe.matmul(MM[bR][:], lhsT=wga[:, c, :], rhs=XCB[u][:], start=True, stop=True),
                  reads=["wga", ("XCB", u)], writes=[("MM", bR)])
            bI = next_mm()
            S_.op("pe", lambda pe: pe.matmul(MM[bI][:], lhsT=wgx[:, c, :], rhs=XCB[u][:], start=True, stop=True),
                  reads=["wgx", ("XCB", u)], writes=[("MM", bI)])
            acts = []

            def A(out, in_, key_r, key_w, extra=(), **kw):
                acts.append(lambda: S_.op("act", lambda a: a.activation(out=out, in_=in_, **kw),
                                          reads=[key_r] + list(extra), writes=[key_w]))
            tr, ti, aa = TR[u][:], TI[u][:], AA[u][:]
            kr, ki, ka = ("TR", u), ("TI", u), ("AA", u)
            A(tr, MM[bR][:], ("MM", bR), kr, ["dv_hba"], func=AF.Exp, scale=-1.0, bias=dv[:, HBA + c:HBA + c + 1])
            A(ti, MM[bI][:], ("MM", bI), ki, ["dv_hbx"], func=AF.Exp, scale=-1.0, bias=dv[:, HBX + c:HBX + c + 1])
            A(tr, tr, kr, kr, func=AF.Ln, scale=1.0, bias=1.0)
            A(tr, tr, kr, kr, func=AF.Exp, scale=-1.0)
            A(aa, tr, kr, ka, ["dv_cch"], func=AF.Exp, scale=dv[:, CCH + c:CCH + c + 1])
            A(tr, tr, kr, kr, ["dv_hc"], func=AF.Exp, scale=dv[:, HC + c:HC + c + 1])
            A(tr, tr, kr, kr, func=AF.Ln, scale=-1.0, bias=1.0)
            A(tr, tr, kr, kr, func=AF.Exp, scale=0.5)
            A(ti, ti, ki, ki, func=AF.Ln, scale=1.0, bias=1.0)
            A(ti, ti, ki, ki, func=AF.Exp, scale=-1.0)
            for f in acts[:5]:
                f()
            rg_half_pending[0] = acts[5:]

        def rg_s3(i):
            c, j = rg_steps[i]
            u = i % NRG
            S_.op("dve", lambda v: v.tensor_tensor(out=TI[u][:], in0=TI[u][:], in1=XCB[u][:], op=ALU.mult),
                  reads=[("TI", u), ("XCB", u)], writes=[("TI", u)])
            S_.op("dve", lambda v: v.tensor_tensor(out=TI[u][:], in0=TI[u][:], in1=TR[u][:], op=ALU.mult),
                  reads=[("TI", u), ("TR", u)], writes=[("TI", u)])
            def s3b():
                S_.op("dve", lambda v: v.tensor_tensor_scan(out=HH[u][:], data0=AA[u][:], data1=TI[u][:],
                                                            initial=hst[:, c:c + 1], op0=ALU.mult, op1=ALU.add),
                      reads=[("AA", u), ("TI", u), ("hst", c)], writes=[("HH", u)])
                S_.op("dve", lambda v: v.tensor_copy(out=hst[:, c:c + 1], in_=HH[u][:, 511:512]),
                      reads=[("HH", u)], writes=[("hst", c)])
                S_.op("dve", lambda v: v.tensor_tensor(out=y[:, c, j * 512:(j + 1) * 512], in0=HH[u][:],
                                                       in1=y[:, c, j * 512:(j + 1) * 512], op=ALU.mult),
                      reads=[("HH", u), ("y", c, j)], writes=[("y", c, j)])
            rg_s3b_pending.append(s3b)

        rg_steps = [(c, j) for j in range(NB) for c in range(4)]

        pt_i = [0]

        def transpose_gate(ychunk, j, scalar_ap):
            b = next_mm()
            pT = MM[b][:].bitcast(BF16)

            def tr(pe):
                ins = None
                for qi in range(4):
                    ins = pe.transpose(pT[:, qi * 128:(qi + 1) * 128], onb[:, qi, :], idb[:])
                return ins
            S_.op("pe", tr, reads=["onb", "idb"], writes=[("MM", b)])
            ysl = y[:, ychunk, j * 512:(j + 1) * 512]
            if scalar_ap is None:
                S_.op("dve", lambda v: v.tensor_tensor(out=ysl, in0=pT[:, 0:512], in1=ysl, op=ALU.mult),
                      reads=[("MM", b), ("y", ychunk, j)], writes=[("y", ychunk, j)])
            else:
                S_.op("dve", lambda v: v.scalar_tensor_tensor(out=ysl, in0=pT[:, 0:512], scalar=scalar_ap, in1=ysl,
                                                              op0=ALU.mult, op1=ALU.mult),
                      reads=[("MM", b), ("y", ychunk, j), "dv_gs"], writes=[("y", ychunk, j)])

        def recip_den(s_, col0):
            for half in range(2):
                S_.op("dve", lambda v, half=half: v.reciprocal(
                    out=rd[:, col0 + 2 * half:col0 + 2 * half + 2], in_=o_view(s_, half)[:, :, 128]),
                    reads=[("MM", o_bank(s_, half))], writes=[("rd", col0, half)])

        post_pending = [None]

        def run_unit(main_fn, post_a, post_b):
            main_fn()
            prev = post_pending[0]
            if prev is not None:
                prev()
            post_a()
            post_pending[0] = post_b

        def post_flush():
            b1_flush()
            if post_pending[0] is not None:
                post_pending[0]()
                post_pending[0] = None

        def xq_evac(b, j):
            S_.op("dve", lambda v: v.tensor_copy(out=xq[:, j * 512:(j + 1) * 512], in_=MM[b][:]),
                  reads=[("MM", b)], writes=[("xq", j)])

        def xg_raw_evac(h1):
            def ev(b, j):
                S_.op("dve", lambda v: v.tensor_copy(out=y[:, 8 + h1, j * 512:(j + 1) * 512], in_=MM[b][:]),
                      reads=[("MM", b)], writes=[("y", 8 + h1, j)])
            return ev

        def xattn_head(h, slq, slg):
            if h == 0:
                mm_pool[0] = [2, 3, 4, 5, 6, 7]
                for j in range(NB):
                    b = proj_fm(slq, h, j)
                    xq_evac(b, j)
                rg_point()
                for j in range(NB):
                    b = proj_fm(slg, h, j)
                    S_.op("act", lambda a, b=b, j=j: a.activation(out=y[:, 8 + h, j * 512:(j + 1) * 512],
                                                                  in_=MM[b][:], func=AF.Silu),
                          reads=[("MM", b)], writes=[("y", 8 + h, j)])
                for c in range(4):
                    S_.op("act", lambda a, c=c: a.activation(out=y[:, c, :], in_=y[:, c, :], func=AF.Silu),
                          reads=[("y", c, j) for j in range(NB)], writes=[("y", c, j) for j in range(NB)])
            else:
                fill_emit(len(fill_q))
                rg_point()
                S_.op("act", lambda a: a.activation(out=y[:, 8 + h, :], in_=y[:, 8 + h, :], func=AF.Silu),
                      reads=[("y", 8 + h, j) for j in range(NB)], writes=[("y", 8 + h, j) for j in range(NB)])
            mm_pool[0] = [6, 7]
            if h + 1 < 4:
                for j in range(NB):
                    fill_q.extend(fm_pieces(slg, h + 1, j, xg_raw_evac(h + 1), 4))

            def mid():
                rg_point()
                fill_emit(8)
            for j in range(NB):
                run_unit(lambda j=j: xattn_main(h, j, mid), lambda j=j: xattn_post_a(h, j),
                         lambda j=j: transpose_gate(8 + h, j, None))
                if h + 1 < 4:
                    fill_q.extend(fm_pieces(slq, h + 1, j, xq_evac, 4))

        def xattn_main(h, j, mid_hook=None):
            if True:
                s_ = 0
                pts = []
                for mt in range(2):
                    S_.op("pe", lambda pe, mt=mt, j=j: pe.matmul(
                        ST[mt][:], lhsT=kmT[:, h, mt * 128:(mt + 1) * 128], rhs=xq[:, j * 512:(j + 1) * 512],
                        start=True, stop=True), reads=[("kmT", h), ("xq", j)], writes=[STK[mt]])
                    p = pt_i[0] % NPT
                    pt_i[0] += 1
                    pts.append(p)
                    S_.op("act", lambda a, mt=mt, p=p: a.activation(out=PT[p][:], in_=ST[mt][:], func=AF.Exp,
                                                                    scale=128.0 ** -0.5),
                          reads=[STK[mt]], writes=[("PT", p)])
                if mid_hook is not None:
                    mid_hook()
                for mt in range(2):
                    p = pts[mt]

                    def pv(pe, mt=mt, p=p, s_=s_):
                        ins = None
                        for qi in range(4):
                            ins = pe.matmul(o_slice(s_, qi), lhsT=PT[p][:, qi * 128:(qi + 1) * 128],
                                            rhs=vm[:, mt, h, :], start=(mt == 0 and qi % 2 == 0), stop=(mt == 1),
                                            skip_group_check=True)
                        return ins
                    S_.op("pe", pv, reads=[("vm", mt), ("PT", p)], writes=o_keys(s_))

        def xattn_post_a(h, j):
            if True:
                s_ = 0
                recip_den(s_, 8)
                for half in range(2):
                    S_.op("dve", lambda v, half=half, s_=s_: v.tensor_tensor(
                        out=onb[:, 2 * half:2 * half + 2, :], in0=o_view(s_, half)[:, :, 0:128],
                        in1=rd[:, 8 + 2 * half:8 + 2 * half + 2].unsqueeze(2).to_broadcast([128, 2, 128]), op=ALU.mult),
                        reads=[("MM", o_bank(s_, half)), ("rd", 8, half)], writes=["onb"])

        fill_q = []
        fill_bank = [0]

        def fill_emit(n):
            for _ in range(n):
                if fill_q:
                    fill_q.pop(0)()

        def q_evac(b, j):
            S_.op("dve", lambda v: v.tensor_copy(out=Q1[0:64, j * 512:(j + 1) * 512], in_=MM[b][0:64, :]),
                  reads=[("MM", b)], writes=[("Q1", j)])
            S_.op("dve", lambda v: v.tensor_copy(out=Q2[64:128, j * 512:(j + 1) * 512], in_=MM[b][64:128, :]),
                  reads=[("MM", b)], writes=[("Q2", j)])

        def k_evac(b, j):
            S_.op("dve", lambda v: v.tensor_copy(out=KK[:, j * 512:(j + 1) * 512], in_=MM[b][:]),
                  reads=[("MM", b)], writes=[("KK", j)])

        def v_evac(b, t4):
            S_.op("dve", lambda v: v.tensor_copy(
                out=VV[:, t4 * 4:(t4 + 1) * 4, 0:128], in_=MM[b][:].rearrange("p (t n) -> p t n", t=4)),
                reads=[("MM", b), ("VV", t4)], writes=[("VV", t4)])

        def fm_pieces(sl, cc, j, evac, npieces):
            ctx = {}
            per = 8 // npieces

            def piece(i):
                def f():
                    if i == 0:
                        ctx["b"] = fill_bank[0] % 2
                        fill_bank[0] += 1
                    b = ctx["b"]

                    def mm(pe):
                        ins = None
                        for k in range(i * per, (i + 1) * per):
                            ins = pe.matmul(MM[b][:], lhsT=wbuf[sl][:, k, cc * 128:(cc + 1) * 128],
                                            rhs=xT[:, k, j * 512:(j + 1) * 512], start=(k == 0), stop=(k == 7))
                        return ins
                    S_.op("pe", mm, reads=wkeys(sl, True) + [("xT", 4 * j + i_) for i_ in range(4)],
                          writes=[("MM", b)])
                    if i == npieces - 1:
                        evac(b, j)
                return f
            return [piece(i) for i in range(npieces)]

        def v_pieces(sl, t4, npieces):
            ctx = {}
            per = 32 // npieces

            def piece(i):
                def f():
                    if i == 0:
                        ctx["b"] = fill_bank[0] % 2
                        fill_bank[0] += 1
                    b = ctx["b"]

                    def mm(pe):
                        ins = None
                        for m in range(i * per, (i + 1) * per):
                            tt, k = m // 8, m % 8
                            t = t4 * 4 + tt
                            ins = pe.matmul(MM[b][:, tt * 128:(tt + 1) * 128], lhsT=xT[:, k, t * 128:(t + 1) * 128],
                                            rhs=wbuf[sl][:, k, 256:384], start=(k == 0), stop=(k == 7))
                        return ins
                    S_.op("pe", mm, reads=wkeys(sl, True) + [("xT", 4 * t4 + i_) for i_ in range(4)],
                          writes=[("MM", b)])
                    if i == npieces - 1:
                        v_evac(b, t4)
                return f
            return [piece(i) for i in range(npieces)]

        def block_pieces(sl, jb, fine):
            n_fm, n_v = (4, 8) if fine else (1, 1)
            return (fm_pieces(sl, 0, jb, q_evac, n_fm) + fm_pieces(sl, 1, jb, k_evac, n_fm) +
                    v_pieces(sl, jb, n_v))

        def g_raw_evac(h1):
            def ev(b, j):
                S_.op("dve", lambda v: v.tensor_copy(out=y[:, 4 + h1, j * 512:(j + 1) * 512], in_=MM[b][:]),
                      reads=[("MM", b)], writes=[("y", 4 + h1, j)])
            return ev

        def gate_pieces(sl, h1):
            out = []
            for j in range(NB):
                out += fm_pieces(sl, 3, j, g_raw_evac(h1), 4)
            return out

        def dattn_proj_upfront(h, sl):
            if h == 0:
                for j in range(NB):
                    b = proj_fm(sl, 3, j, head=True)
                    S_.op("act", lambda a, b=b, j=j: a.activation(out=y[:, 4 + h, j * 512:(j + 1) * 512],
                                                                  in_=MM[b][:], func=AF.Silu),
                          reads=[("MM", b)], writes=[("y", 4 + h, j)])
            else:
                S_.op("act", lambda a: a.activation(out=y[:, 4 + h, :], in_=y[:, 4 + h, :], func=AF.Silu),
                      reads=[("y", 4 + h, j) for j in range(NB)], writes=[("y", 4 + h, j) for j in range(NB)])
            for p in block_pieces(sl, 0, False):
                p()

        def dattn_unit(h, j, w, fill_rate=0):
            Qw = Q1 if w == 0 else Q2
            qk = ("Q1", j) if w == 0 else ("Q2", j)
            nk = 4 * j + 4
            s_ = 0
            state = {}

            def qk_mm(kt):
                c0 = max(0, kt - 4 * j) * 128
                sb_ = kt % NST

                def f(pe):
                    diag = kt >= 4 * j
                    ins = pe.matmul(ST[sb_][:, c0:512], lhsT=KK[:, kt * 128:(kt + 1) * 128],
                                    rhs=Qw[:, j * 512 + c0:(j + 1) * 512], start=True, stop=not diag)
                    if diag:
                        ins = pe.matmul(ST[sb_][:, c0:c0 + 128], lhsT=idb[:], rhs=maskb[:], start=False, stop=True)
                    return ins
                S_.op("pe", f, reads=[("KK", kt // 4), qk, "idb", "maskb"], writes=[STK[sb_]])

            def exp_step(kt):
                c0 = max(0, kt - 4 * j) * 128
                sb_ = kt % NST
                p = pt_i[0] % NPT
                pt_i[0] += 1
                state[kt] = p
                S_.op("act", lambda a: a.activation(out=PT[p][:, c0:512], in_=ST[sb_][:, c0:512], func=AF.Exp,
                                                    scale=0.125),
                      reads=[STK[sb_]], writes=[("PT", p)])

            def pv_mm(kt):
                i0 = max(0, kt - 4 * j)
                p = state[kt]

                def f(pe):
                    ins = None
                    for qi in range(i0, 4):
                        ins = pe.matmul(o_slice(s_, qi), lhsT=PT[p][:, qi * 128:(qi + 1) * 128], rhs=VV[:, kt, :],
                                        start=(kt == 0 and qi % 2 == 0), stop=(kt == 4 * j + qi),
                                        skip_group_check=True)
                    return ins
                S_.op("pe", f, reads=[("VV", kt // 4), ("PT", p)], writes=o_keys(s_))

            qk_mm(0)
            qk_mm(1)
            if fill_rate > 0:
                fill_emit(5 if (j < NB - 1 or w == 0) else 2)
            for kt in range(nk):
                if kt + 2 < nk:
                    qk_mm(kt + 2)
                exp_step(kt)
                if j < NB - 1 or kt % 2 == 1:
                    fill_emit(fill_rate)
                pv_mm(kt)
                if kt == 1:
                    b1_flush()

        b1_pending = [None]

        def b1_flush():
            if b1_pending[0] is not None:
                f = b1_pending[0]
                b1_pending[0] = None
                f()

        def dattn_post_a(h, j, w):
            recip_den(0, 4 * w)
            if w == 0:
                for half in range(2):
                    S_.op("dve", lambda v, half=half: v.tensor_tensor(
                        out=O1n[:, 2 * half:2 * half + 2, :], in0=o_view(0, half)[:, :, 0:128],
                        in1=rd[:, 2 * half:2 * half + 2].unsqueeze(2).to_broadcast([128, 2, 128]), op=ALU.mult),
                        reads=[("MM", o_bank(0, half)), ("rd", 0, half)], writes=[("O1n", half)])
            else:
                S_.op("dve", lambda v: v.tensor_scalar(out=rd[:, 4:8], in0=rd[:, 4:8], scalar1=dv[:, NLAM:NLAM + 1],
                                                       scalar2=None, op0=ALU.mult),
                      reads=[("rd", 4, 0), ("rd", 4, 1), "dv_nlam"], writes=[("rd", 4, 0), ("rd", 4, 1)])
                for qi in range(4):
                    S_.op("dve", lambda v, qi=qi: v.scalar_tensor_tensor(
                        out=oc[:, qi, :], in0=o_view(0, qi // 2)[:, qi % 2, 0:128], scalar=rd[:, 4 + qi:5 + qi],
                        in1=O1n[:, qi, :], op0=ALU.mult, op1=ALU.add),
                        reads=[("MM", o_bank(0, qi // 2)), ("rd", 4, qi // 2), ("O1n", qi // 2)], writes=[("oc", qi)])
                def b1():
                    ock = [("oc", qi) for qi in range(4)]
                    S_.op("dve", lambda v: v.tensor_tensor(out=sq[:], in0=oc[:], in1=oc[:], op=ALU.mult),
                          reads=ock, writes=["sq"])
                    S_.op("dve", lambda v: v.reduce_sum(out=rd[:, 12:16], in_=sq[:], axis=AX.X), reads=["sq"],
                          writes=["rstd4"])
                    S_.op("act", lambda a: a.activation(out=rd[:, 12:16], in_=rd[:, 12:16], func=AF.Ln,
                                                        scale=1.0 / 128.0, bias=EPS), reads=["rstd4"], writes=["rstd4"])
                    S_.op("act", lambda a: a.activation(out=rd[:, 12:16], in_=rd[:, 12:16], func=AF.Exp, scale=-0.5),
                          reads=["rstd4"], writes=["rstd4"])
                    S_.op("dve", lambda v: v.tensor_tensor(out=onb[:], in0=oc[:],
                                                           in1=rd[:, 12:16].unsqueeze(2).to_broadcast([128, 4, 128]),
                                                           op=ALU.mult), reads=ock + ["rstd4"], writes=["onb"])
                b1_pending[0] = b1

        def dattn_post_b(h, j, w):
            if w == 0:
                return
            transpose_gate(4 + h, j, dv[:, GS2:GS2 + 1])

        NRU = len(rg_steps)
        rg_next = [0]

        def rg_some(n):
            for _ in range(n):
                rg_half()
                i = rg_next[0]
                if i >= NRU + 2:
                    return
                if 0 <= i - 2 < NRU:
                    rg_s3(i - 2)
                if 0 <= i - 1 < NRU:
                    rg_s2(i - 1)
                if i < NRU:
                    rg_s1(i)
                rg_next[0] += 1

        rg_pt = [0]

        def rg_point():
            if rg_pt[0] % 2 == 0:
                rg_some(1)
            else:
                rg_half()
            rg_pt[0] += 1

        slq = load_wgroup_cols(3072)
        slg = load_wgroup_cols(3584)
        for h in range(4):
            xattn_head(h, slq, slg)
        fill_emit(len(fill_q))
        sl_next = load_wgroup_head(0)
        for h in range(4):
            sl = sl_next
            mm_pool[0] = [2, 3, 4, 5, 6, 7]
            rg_some(1)
            dattn_proj_upfront(h, sl)
            rg_half()
            mm_pool[0] = [7]
            if h + 1 < 4:
                sl_next = load_wgroup_head(h + 1)
            for j in range(NB):
                rate = 0
                steps = 2 * (4 * j + 4)
                if j + 1 < NB:
                    fill_q.extend(block_pieces(sl, j + 1, True))
                    rate = max(1, -(-(len(fill_q) - 10) // steps))
                elif h + 1 < 4:
                    fill_q.extend(gate_pieces(sl_next, h + 1))
                    rate = 1
                for w in range(2):
                    run_unit(lambda h=h, j=j, w=w, rate=rate: dattn_unit(h, j, w, rate),
                             lambda h=h, j=j, w=w: dattn_post_a(h, j, w),
                             lambda h=h, j=j, w=w: dattn_post_b(h, j, w))
                fill_emit(len(fill_q))
        post_flush()

        mm_pool[0] = [0, 1, 2, 3, 4, 5, 6, 7]
        mmi[0] = 0
        all_xT = [("xT", t) for t in range(NT)]
        wout_v = wout_d.rearrange("(c p) n -> p c n", p=128)
        for cg in range(3):
            S_.dma("pool", lambda q, cg=cg: q.dma_start(out=wo[:, cg * 4:(cg + 1) * 4, :],
                                                         in_=wout_v[:, cg * 4:(cg + 1) * 4, :]),
                   writes=[("wo", cg)] + (all_xT if cg == 0 else []), reads=all_xT if cg > 0 else [])
        out_tokens = []

        def out_stage1(t):
            sl = xs_i[0] % NXS
            xs_i[0] += 1
            par = t % 2
            c0 = 8 + 3 * par
            S_.dma("sp", lambda q: q.dma_start(out=xs[sl][:], in_=x_d[t * 128:(t + 1) * 128, :]),
                   writes=[("xs", sl)])
            bb = [next_mm(), next_mm()]
            for half in range(2):
                b = bb[half]

                def mmo(pe, b=b, half=half):
                    ins = None
                    for c in range(12):
                        ins = pe.matmul(MM[b][:], lhsT=y[:, c, t * 128:(t + 1) * 128],
                                        rhs=wo[:, c, half * 512:(half + 1) * 512], start=(c == 0), stop=(c == 11))
                    return ins
                S_.op("pe", mmo, reads=[("wo", 0), ("wo", 1), ("wo", 2)] + [("y", c, t // 4) for c in range(12)],
                      writes=[("MM", b)])
                S_.op("act", lambda a, b=b, half=half: a.activation(
                    out=sqb[:], in_=MM[b][:], func=AF.Square, accum_out=sst[:, c0 + half:c0 + half + 1]),
                    reads=[("MM", b)], writes=["sqb", ("oss", par, half)])
            S_.op("dve", lambda v: v.tensor_tensor(out=sst[:, c0 + 2:c0 + 3], in0=sst[:, c0:c0 + 1],
                                                   in1=sst[:, c0 + 1:c0 + 2], op=ALU.add),
                  reads=[("oss", par, 0), ("oss", par, 1)], writes=[("oss2", par)])
            S_.op("act", lambda a: a.activation(out=sst[:, c0 + 2:c0 + 3], in_=sst[:, c0 + 2:c0 + 3], func=AF.Ln,
                                                scale=1.0 / D, bias=EPS), reads=[("oss2", par)], writes=[("oss2", par)])
            S_.op("act", lambda a: a.activation(out=sst[:, c0 + 2:c0 + 3], in_=sst[:, c0 + 2:c0 + 3], func=AF.Exp,
                                                scale=-0.5), reads=[("oss2", par)], writes=[("oss2", par)])
            return sl, bb, par, c0

        def out_stage2(t, sl, bb, par, c0):
            for half in range(2):
                b = bb[half]
                S_.op("dve", lambda v, b=b, half=half: v.scalar_tensor_tensor(
                    out=resb[:, half * 512:(half + 1) * 512], in0=MM[b][:], scalar=sst[:, c0 + 2:c0 + 3],
                    in1=pk[:, GPOST + half * 512:GPOST + (half + 1) * 512], op0=ALU.mult, op1=ALU.mult),
                    reads=[("MM", b), ("oss2", par), "pk"],
                    writes=[("resb", half)] + [("Q1", jj) for jj in range(NB)])
            S_.op("dve", lambda v: v.tensor_tensor(out=xs[sl][:], in0=resb[:], in1=xs[sl][:], op=ALU.add),
                  reads=[("resb", 0), ("resb", 1), ("xs", sl)], writes=[("xs", sl)])
            tok = S_.dma("sp", lambda q: q.dma_start(out=out_d[t * 128:(t + 1) * 128, :], in_=xs[sl][:]),
                         reads=[("xs", sl)])
            out_tokens.append(tok)

        prev_ctx = None
        for t in range(NT):
            if t < NT // 2:
                if t % 2 == 0:
                    rg_some(1)
                else:
                    rg_half()
            elif t == NT // 2:
                rg_some(NRU + 2)
            ctx_t = out_stage1(t)
            if prev_ctx is not None:
                out_stage2(t - 1, *prev_ctx)
            prev_ctx = ctx_t
            if t >= NT // 2:
                rg_half()
        out_stage2(NT - 1, *prev_ctx)
        S_.wait_all("sp", out_tokens)
        S_.emit(block)
    return nc


def _host_consts(inp):
    f = np.float32
    pk = np.zeros((128, NPK), f)
    pk[:, 0:8] = inp["g_pre"].reshape(8, 128).T
    pk[:, 8:16] = inp["g_mem"].reshape(8, 128).T
    cw = inp["conv_w"].reshape(4, 4, 128)
    pk[:, 16:32] = cw.transpose(2, 1, 0).reshape(128, 16)
    pk[:, 32:36] = inp["conv_b"].reshape(4, 128).T
    pk[:, 36:40] = inp["b_rg_a"].reshape(4, 128).T
    pk[:, 40:44] = inp["b_rg_x"].reshape(4, 128).T
    pk[:, 44:48] = inp["lru_lambda"].reshape(4, 128).T
    pk[:, 48] = inp["g_subln"]
    pk[:, 64:128] = inp["lambda_q1"][None, :]
    pk[:, 128:192] = inp["lambda_k1"][None, :]
    pk[:, 192:256] = inp["lambda_q2"][None, :]
    pk[:, 256:320] = inp["lambda_k2"][None, :]
    pk[:, 320:1344] = inp["g_post"][None, :]
    pk[:, 1344:2368] = inp["g_pre"][None, :]
    pk[:, 2368:3392] = inp["g_mem"][None, :]

    def blockdiag(w):
        o = np.zeros((128, 4, 128), f)
        for c in range(4):
            o[0:64, c, 0:64] = w[2 * c]
            o[64:128, c, 64:128] = w[2 * c + 1]
        return o
    ident = np.eye(128, dtype=f)
    mask = np.where(np.triu(np.ones((128, 128), f)) > 0, 0.0, -30000.0).astype(f)
    return pk, blockdiag(inp["w_rg_a"]), blockdiag(inp["w_rg_x"]), ident, mask


_NC_CACHE = {}


def kernel(**inputs):
    inp = {k: np.asarray(v) for k, v in inputs.items()}
    x = np.ascontiguousarray(inp["x"], dtype=np.float32)
    B, S, _ = x.shape
    pk, wga, wgx, ident, mask = _host_consts(inp)
    if S not in _NC_CACHE:
        _NC_CACHE[S] = build_nc(S)
    nc = _NC_CACHE[S]
    shared = dict(w_in=np.ascontiguousarray(inp["w_in"], dtype=np.float32),
                  w_kv=np.ascontiguousarray(inp["w_mem_kv"], dtype=np.float32),
                  w_out=np.ascontiguousarray(inp["w_out"], dtype=np.float32),
                  pk=pk, wga=wga, wgx=wgx, ident=ident, mask=mask)
    in_maps = []
    for b in range(B):
        m = dict(shared)
        m["x"] = x[b]
        m["mem"] = np.ascontiguousarray(inp["mem"][b], dtype=np.float32)
        in_maps.append(m)
    res = run_bass_kernel_spmd(nc, in_maps, core_ids=list(range(B)))
    return np.stack([np.asarray(r["out"]) for r in res.results], axis=0).astype(np.float32)
```
